# Optimizing a Trainium2 kernel written in Bass

```python
import math
import jax, jax.numpy as jnp
from jax import lax
import numpy as np

D_MODEL = 4096
BATCH = 4
SEQ = 4096
DEPTH = 4

N_HEADS = 32
QK_NOPE = 128
QK_ROPE = 64
V_HEAD = 128
Q_LORA = 1024
KV_LORA = 512
MLA_WIDTH = N_HEADS * V_HEAD
QK_HEAD = QK_NOPE + QK_ROPE
ATTN_SCALE = 1.0 / math.sqrt(QK_HEAD)
ROPE_THETA = 10000.0
Q_BLOCK = 128
CONV_WIDTH = D_MODEL
CONV_K = 31
EPS = 1e-6
OFF_CQ = 0
OFF_CKV = OFF_CQ + Q_LORA
OFF_KR = OFF_CKV + KV_LORA
OFF_GMLA = OFF_KR + QK_ROPE
OFF_CONV = OFF_GMLA + MLA_WIDTH
OFF_GCONV = OFF_CONV + 2 * CONV_WIDTH
OFF_MA = OFF_GCONV + CONV_WIDTH
OFF_MC = OFF_MA + D_MODEL
IN_WIDTH = OFF_MC + D_MODEL

kernel_name = "mla_conformer_gated_hybrid"


def rms_norm(x, g):
    xf = x.astype(jnp.float32)
    y = xf * lax.rsqrt(jnp.mean(xf * xf, axis=-1, keepdims=True) + EPS)
    return (y * g.astype(jnp.float32)).astype(x.dtype)


def layer_norm(x, g, b):
    xf = x.astype(jnp.float32)
    mu = jnp.mean(xf, axis=-1, keepdims=True)
    xc = xf - mu
    var = jnp.mean(xc * xc, axis=-1, keepdims=True)
    y = xc * lax.rsqrt(var + EPS) * g.astype(jnp.float32) + b.astype(jnp.float32)
    return y.astype(x.dtype)


def rope_tables(positions, dtype):
    inv_freq = 1.0 / (ROPE_THETA ** (jnp.arange(0, QK_ROPE, 2, dtype=jnp.float32) / QK_ROPE))
    ang = positions.astype(jnp.float32)[..., None] * inv_freq
    return jnp.cos(ang).astype(dtype), jnp.sin(ang).astype(dtype)


def apply_rope(x, cos, sin):
    x1, x2 = jnp.split(x, 2, axis=-1)
    return jnp.concatenate([x1 * cos - x2 * sin, x2 * cos + x1 * sin], axis=-1)


def causal_mla_attention(q_nope, q_rope, k_nope, k_rope, v):
    b, s, h, _ = q_nope.shape
    nb = s // Q_BLOCK
    qn = q_nope.reshape(b, nb, Q_BLOCK, h, QK_NOPE).transpose(1, 0, 2, 3, 4)
    qr = q_rope.reshape(b, nb, Q_BLOCK, h, QK_ROPE).transpose(1, 0, 2, 3, 4)
    key_idx = jnp.arange(s)

    def one_block(args):
        qn_b, qr_b, blk = args
        sc = (jnp.einsum('bqhd,bkhd->bhqk', qn_b, k_nope)
              + jnp.einsum('bqhr,bkr->bhqk', qr_b, k_rope)).astype(jnp.float32) * ATTN_SCALE
        q_idx = blk * Q_BLOCK + jnp.arange(Q_BLOCK)
        mask = key_idx[None, :] <= q_idx[:, None]
        sc = jnp.where(mask[None, None], sc, -jnp.inf)
        p = jax.nn.softmax(sc, axis=-1).astype(v.dtype)
        return jnp.einsum('bhqk,bkhd->bqhd', p, v)

    out = lax.map(one_block, (qn, qr, jnp.arange(nb)))
    return out.transpose(1, 0, 2, 3, 4).reshape(b, s, h * V_HEAD)


def causal_depthwise_conv(u, w, bias):
    y = lax.conv_general_dilated(
        u, w[:, None, :].astype(u.dtype), window_strides=(1,),
        padding=[(CONV_K - 1, 0)], dimension_numbers=('NWC', 'WIO', 'NWC'),
        feature_group_count=u.shape[-1])
    return y + bias


def setup_inputs(seed: int = 0) -> dict:
    key = jax.random.key(seed)
    ks = jax.random.split(key, 20)
    f32 = jnp.float32

    def w(k, shape, fan_in):
        return jax.random.normal(k, shape, f32) * (fan_in ** -0.5)

    def gain(k, shape):
        return 1.0 + 0.01 * jax.random.normal(k, shape, f32)

    def bias(k, shape):
        return 0.01 * jax.random.normal(k, shape, f32)

    x = jax.random.normal(ks[0], (BATCH, SEQ, D_MODEL), f32)
    offs = jax.random.randint(ks[1], (BATCH, 1), 0, 4096, dtype=jnp.int32)
    positions = (jnp.arange(SEQ, dtype=jnp.int32)[None, :] + offs).astype(jnp.int32)
    return {
        "x": x,
        "positions": positions,
        "g_pre": gain(ks[2], (DEPTH, D_MODEL)),
        "w_in": w(ks[3], (DEPTH, D_MODEL, IN_WIDTH), D_MODEL),
        "g_q": gain(ks[4], (DEPTH, Q_LORA)),
        "w_q_up": w(ks[5], (DEPTH, Q_LORA, N_HEADS * QK_HEAD), Q_LORA),
        "g_kv": gain(ks[6], (DEPTH, KV_LORA)),
        "w_kv_up": w(ks[7], (DEPTH, KV_LORA, N_HEADS * (QK_NOPE + V_HEAD)), KV_LORA),
        "w_o_mla": w(ks[8], (DEPTH, MLA_WIDTH, D_MODEL), MLA_WIDTH),
        "w_dw": w(ks[9], (DEPTH, CONV_K, CONV_WIDTH), CONV_K),
        "b_dw": bias(ks[10], (DEPTH, CONV_WIDTH)),
        "g_cn": gain(ks[11], (DEPTH, CONV_WIDTH)),
        "b_cn": bias(ks[12], (DEPTH, CONV_WIDTH)),
        "w_pw_out": w(ks[13], (DEPTH, CONV_WIDTH, D_MODEL), CONV_WIDTH),
        "w_out": w(ks[14], (DEPTH, D_MODEL, D_MODEL), D_MODEL),
        "g_final": gain(ks[15], (D_MODEL,)),
    }


def reference(x, positions, g_pre, w_in, g_q, w_q_up, g_kv, w_kv_up, w_o_mla,
              w_dw, b_dw, g_cn, b_cn, w_pw_out, w_out, g_final):
    b, s, _ = x.shape
    cos, sin = rope_tables(positions, x.dtype)
    cos_h, sin_h = cos[:, :, None, :], sin[:, :, None, :]
    for l in range(DEPTH):
        h = rms_norm(x, g_pre[l])
        z = h @ w_in[l]
        c_q = z[..., OFF_CQ:OFF_CKV]
        c_kv = z[..., OFF_CKV:OFF_KR]
        k_rope = z[..., OFF_KR:OFF_GMLA]
        gate_mla = z[..., OFF_GMLA:OFF_CONV]
        conv_in = z[..., OFF_CONV:OFF_GCONV]
        gate_conv = z[..., OFF_GCONV:OFF_MA]
        merge_a = z[..., OFF_MA:OFF_MC]
        merge_c = z[..., OFF_MC:IN_WIDTH]

        q = (rms_norm(c_q, g_q[l]) @ w_q_up[l]).reshape(b, s, N_HEADS, QK_HEAD)
        q_nope, q_rope = q[..., :QK_NOPE], q[..., QK_NOPE:]
        kv = (rms_norm(c_kv, g_kv[l]) @ w_kv_up[l]).reshape(b, s, N_HEADS, QK_NOPE + V_HEAD)
        k_nope, v = kv[..., :QK_NOPE], kv[..., QK_NOPE:]
        q_rope = apply_rope(q_rope, cos_h, sin_h)
        k_rope = apply_rope(k_rope, cos, sin)
        attn = causal_mla_attention(q_nope, q_rope, k_nope, k_rope, v)
        y_a = (attn * jax.nn.silu(gate_mla)) @ w_o_mla[l]

        u_val, u_gate = conv_in[..., :CONV_WIDTH], conv_in[..., CONV_WIDTH:]
        u = u_val * jax.nn.sigmoid(u_gate)
        u = causal_depthwise_conv(u, w_dw[l], b_dw[l])
        u = jax.nn.silu(layer_norm(u, g_cn[l], b_cn[l]))
        y_c = (u * jax.nn.silu(gate_conv)) @ w_pw_out[l]

        y = jax.nn.sigmoid(merge_a) * y_a + jax.nn.sigmoid(merge_c) * y_c
        x = x + y @ w_out[l]
    return rms_norm(x, g_final)
```

```python
import math
import types
from contextlib import ExitStack
import numpy as np
import ml_dtypes
import concourse.bass as bass
import concourse.mybir as mybir
from concourse.bass_utils import run_bass_kernel_spmd

F32 = mybir.dt.float32
BF16 = mybir.dt.bfloat16
I32 = mybir.dt.int32
AF = mybir.ActivationFunctionType
ALU = mybir.AluOpType
EPS = 1e-6
NCORES = 4


class Cfg:
    def __init__(self, D=4096, S=4096, NH=32, QL=1024, KVL=512, DEPTH=4, CK=31, BATCH=4, ncores=NCORES):
        self.D, self.S, self.NH, self.QL, self.KVL, self.DEPTH, self.CK = D, S, NH, QL, KVL, DEPTH, CK
        self.BATCH = BATCH
        self.ncores = ncores
        self.NS = BATCH // ncores
        self.KC = D // 128
        self.QC = QL // 128
        self.KVC = KVL // 128
        self.TB1 = min(1024, S)
        self.NSB = self.TB1 // 512
        self.NTB1 = S // self.TB1
        self.NB = S // 512
        assert NH * 128 == D
        self.OFF_CQ = 0
        self.OFF_CKV = QL
        self.OFF_KR = QL + KVL
        self.OFF_GM = self.OFF_KR + 64
        self.OFF_CV = self.OFF_GM + D
        self.OFF_GC = self.OFF_CV + 2 * D
        self.OFF_MA = self.OFF_GC + D
        self.OFF_MC = self.OFF_MA + D
        self.INW = self.OFF_MC + D
        ch = []
        for i in range(self.QC):
            ch.append(("cq", i, self.OFF_CQ + 128 * i))
        for i in range(self.KVC):
            ch.append(("ckv", i, self.OFF_CKV + 128 * i))
        ch.append(("kr", 0, self.OFF_KR))
        for i in range(self.KC):
            ch.append(("gm", i, self.OFF_GM + 128 * i))
        for i in range(self.KC):
            ch.append(("uv", i, self.OFF_CV + 128 * i))
            ch.append(("ug", i, self.OFF_CV + D + 128 * i))
        for i in range(self.KC):
            ch.append(("gc", i, self.OFF_GC + 128 * i))
        for i in range(self.KC):
            ch.append(("ma", i, self.OFF_MA + 128 * i))
        for i in range(self.KC):
            ch.append(("mc", i, self.OFF_MC + 128 * i))
        self.chunks = ch
        self.NCH = len(ch)
        o = 0
        self.V_INVF = o; o += 1
        self.V_GFIN = o; o += self.KC
        self.V_L = o
        self.V_GPRE = 0
        self.V_GQ = self.KC
        self.V_GKV = self.V_GQ + self.QC
        self.V_BDW = self.V_GKV + self.KVC
        self.V_GCN = self.V_BDW + self.KC
        self.V_BCN = self.V_GCN + self.KC
        self.V_PER = self.V_BCN + self.KC
        self.NV = self.V_L + DEPTH * self.V_PER


def _chunked(w, cols_list):
    K = w.shape[0]
    out = np.empty((len(cols_list), 128, K // 128, 128), np.float32)
    for j, cols in enumerate(cols_list):
        blk = w[:, cols]
        out[j] = blk.reshape(K // 128, 128, 128).transpose(1, 0, 2)
    return out


def host_consts(cfg):
    ident = np.eye(128, dtype=np.float32).astype(ml_dtypes.bfloat16)
    perm = np.zeros((128, 128), np.float32)
    for b in (0, 64):
        for i in range(32):
            perm[b + i + 32, b + i] = -1.0
            perm[b + i, b + i + 32] = 1.0
    masks = np.zeros((128, 4, 512), np.float32)
    p = np.arange(128)[:, None]
    f = np.arange(512)[None, :]
    for d in range(4):
        masks[:, d, :] = (d * 128 + p <= f)
    masks = masks.astype(ml_dtypes.bfloat16)
    ones_b = np.ones((128, 128), np.float32).astype(ml_dtypes.bfloat16)
    ones_f = np.ones((128, 128), np.float32)
    zeros_b = np.zeros((128, 32), np.float32).astype(ml_dtypes.bfloat16)
    return dict(ident=ident, perm=perm, masks=masks, ones_b=ones_b, ones_f=ones_f, zeros_b=zeros_b)


def host_prep(cfg, inp):
    c = cfg
    L = c.DEPTH
    ar = np.arange(128)
    sh = {}
    cols_in = []
    for kind, i, off in c.chunks:
        if kind == "kr":
            cols_in.append(off + (ar % 64))
        else:
            cols_in.append(off + ar)
    sh["win"] = np.stack([_chunked(np.asarray(inp["w_in"][l]), cols_in) for l in range(L)])
    cols_q = [h * 192 + ar for h in range(c.NH)]
    for j in range(c.NH // 2):
        cols_q.append(np.where(ar < 64, (2 * j) * 192 + 128 + ar, (2 * j + 1) * 192 + 128 + (ar - 64)))
    sh["wq"] = np.stack([_chunked(np.asarray(inp["w_q_up"][l]), cols_q) for l in range(L)])
    cols_k = [h * 256 + ar for h in range(c.NH)]
    sh["wk"] = np.stack([_chunked(np.asarray(inp["w_kv_up"][l]), cols_k) for l in range(L)])
    vcols = np.concatenate([h * 256 + 128 + ar for h in range(c.NH)])
    wv = np.stack([np.asarray(inp["w_kv_up"][l])[:, vcols] for l in range(L)])
    sh["wv"] = np.ascontiguousarray(wv.reshape(L, c.KVC, 128, c.D).transpose(0, 2, 1, 3))
    std = [ar + 128 * i for i in range(c.KC)]
    sh["wo"] = np.stack([_chunked(np.asarray(inp["w_o_mla"][l]), std) for l in range(L)])
    sh["wp"] = np.stack([_chunked(np.asarray(inp["w_pw_out"][l]), std) for l in range(L)])
    sh["wout"] = np.stack([_chunked(np.asarray(inp["w_out"][l]), std) for l in range(L)])
    vec = np.zeros((128, c.NV), np.float32)
    invf = (1.0 / (10000.0 ** (np.arange(0, 64, 2, dtype=np.float32) / np.float32(64.0)))).astype(np.float32)
    vec[:, c.V_INVF] = invf[ar % 32]

    def fm(v):
        return np.asarray(v, np.float32).reshape(-1, 128).T

    vec[:, c.V_GFIN:c.V_GFIN + c.KC] = fm(inp["g_final"])
    for l in range(L):
        b = c.V_L + l * c.V_PER
        vec[:, b + c.V_GPRE:b + c.V_GPRE + c.KC] = fm(inp["g_pre"][l])
        vec[:, b + c.V_GQ:b + c.V_GQ + c.QC] = fm(inp["g_q"][l])
        vec[:, b + c.V_GKV:b + c.V_GKV + c.KVC] = fm(inp["g_kv"][l])
        vec[:, b + c.V_BDW:b + c.V_BDW + c.KC] = fm(inp["b_dw"][l])
        vec[:, b + c.V_GCN:b + c.V_GCN + c.KC] = fm(inp["g_cn"][l])
        vec[:, b + c.V_BCN:b + c.V_BCN + c.KC] = fm(inp["b_cn"][l])
    sh["vec"] = vec
    wdw = np.asarray(inp["w_dw"], np.float32)
    sh["wdw"] = np.ascontiguousarray(wdw.reshape(L, c.CK, c.KC, 128).transpose(0, 3, 2, 1))
    sh.update(host_consts(c))
    for k in ("win", "wq", "wk", "wo", "wp", "wout"):
        a = sh[k]
        sh[k] = a.reshape(a.shape[0], a.shape[1], 128, a.shape[3] * 128)
    sh["wv"] = sh["wv"].reshape(L, 128, c.KVC * c.D)
    sh["wdw"] = sh["wdw"].reshape(L, 128, c.KC * c.CK)
    sh["masks"] = sh["masks"].reshape(128, 4 * 512)
    x = np.asarray(inp["x"], np.float32)
    pos = np.asarray(inp["positions"], np.int32)
    per = []
    for core in range(c.ncores):
        seqs = range(core * c.NS, (core + 1) * c.NS)
        xT = np.stack([np.ascontiguousarray(x[b].T) for b in seqs])
        pr = np.stack([np.broadcast_to(pos[b][None, :], (128, c.S)).copy() for b in seqs])
        d = dict(sh)
        d["xT"] = xT
        d["pos"] = pr
        per.append(d)
    return per


class Res:
    __slots__ = ("w", "r")

    def __init__(self):
        self.w = None
        self.r = {}


class Slot:
    def __init__(self, t, sem_key):
        self.t = t
        self.sem = sem_key
        self.res = Res()


class Sched:
    ENGS = ("sp", "act", "pool", "dve", "pe")

    def __init__(self, nc, es):
        self.nc = nc
        self.es = es
        self.ops = {e: [] for e in self.ENGS}
        self.cnt = {}
        self.sems = {}
        self.waited = {e: {} for e in self.ENGS}
        for e in self.ENGS:
            self.new_sem("E_" + e)
        self.dres = {}

    def new_sem(self, key):
        self.sems[key] = self.es.enter_context(self.nc.semaphore(key))
        self.cnt[key] = 0
        return key

    def R(self, *key):
        r = self.dres.get(key)
        if r is None:
            r = self.dres[key] = Res()
        return r

    def _waits(self, eng, reads, writes):
        need = {}

        def add(tok):
            if tok is None:
                return
            k, v = tok
            if need.get(k, 0) < v:
                need[k] = v
        for r in reads:
            add(r.w)
        for w in writes:
            add(w.w)
            for k, v in w.r.items():
                add((k, v))
        out = []
        wd = self.waited[eng]
        for k, v in need.items():
            if eng == "pe" and k == "E_pe":
                continue
            if wd.get(k, 0) >= v:
                continue
            wd[k] = v
            out.append((k, v))
        return out

    def _commit(self, tok, reads, writes):
        k, v = tok
        for r in reads:
            if r.r.get(k, 0) < v:
                r.r[k] = v
        for w in writes:
            w.w = tok
            w.r = {}

    @staticmethod
    def _freeze(fn):
        if fn is None or not getattr(fn, "__closure__", None):
            return fn
        cells = []
        for c in fn.__closure__:
            try:
                cells.append(types.CellType(c.cell_contents))
            except ValueError:
                cells.append(c)
        nf = types.FunctionType(fn.__code__, fn.__globals__, fn.__name__, fn.__defaults__, tuple(cells))
        nf.__kwdefaults__ = fn.__kwdefaults__
        return nf

    def op(self, eng, fn, r=(), w=()):
        fn = self._freeze(fn)
        waits = self._waits(eng, r, w)
        k = "E_" + eng
        self.cnt[k] += 1
        tok = (k, self.cnt[k])
        self.ops[eng].append((waits, fn, k, 1))
        self._commit(tok, r, w)
        return tok

    def dma(self, q, out_ap, in_ap, semkey, r=(), w=()):
        return self.dmas(q, [(out_ap, in_ap)], semkey, r=r, w=w)

    def dmas(self, q, pairs, semkey, r=(), w=()):
        waits = self._waits(q, r, w)
        for i, (o, i_) in enumerate(pairs):
            self.cnt[semkey] += 16
            self.ops[q].append((waits if i == 0 else [], (lambda e, o=o, i_=i_: e.dma_start(out=o, in_=i_)),
                                semkey, 16))
        tok = (semkey, self.cnt[semkey])
        self._commit(tok, r, w)
        return tok

    def retag(self, res_list, semkey):
        tok = (semkey, self.cnt[semkey])
        for r in res_list:
            r.w = tok

    def wait_all(self, eng, res_list):
        waits = self._waits(eng, res_list, ())
        self.ops[eng].append((waits, None, None, 0))

    def emit(self):
        nc = self.nc
        sems = self.sems
        ops = self.ops

        def replay(name, e):
            for waits, fn, k, inc in ops[name]:
                for (wk, wv) in waits:
                    e.wait_ge(sems[wk], wv)
                if fn is not None:
                    ins = fn(e)
                    ins.then_inc(sems[k], inc)
        with nc.Block() as block:
            @block.sync
            def _(e):
                replay("sp", e)

            @block.scalar
            def _(e):
                replay("act", e)

            @block.gpsimd
            def _(e):
                replay("pool", e)

            @block.vector
            def _(e):
                replay("dve", e)

            @block.tensor
            def _(e):
                replay("pe", e)


class Pool_:
    def __init__(self, S, name, n, shape, dt):
        self.slots = []
        for i in range(n):
            t = S.es.enter_context(S.nc.sbuf_tensor(f"{name}{i}", shape, dt))
            self.slots.append(Slot(t[:], S.new_sem(f"D_{name}{i}")))
        self.i = 0

    def next(self):
        s = self.slots[self.i % len(self.slots)]
        self.i += 1
        return s


def inherit(dst, srcs):
    for s in srcs:
        toks = list(s.res.r.items())
        if s.res.w is not None:
            toks.append(s.res.w)
        for k, v in toks:
            if dst.res.r.get(k, 0) < v:
                dst.res.r[k] = v


class G:
    pass


def build_program(cfg):
    c = cfg
    D, S_, KC, L, NS = c.D, c.S, c.KC, c.DEPTH, c.NS
    TB1, NSB = c.TB1, c.NSB
    nc = bass.Bass("TRN2", target_bir_lowering=False)
    es = ExitStack()
    S = Sched(nc, es)
    g = G()
    g.c, g.S, g.nc = c, S, nc

    def din(name, shape, dt=F32):
        return nc.dram_tensor(name, shape, dt, kind="ExternalInput").ap()

    def dscr(name, shape, dt=BF16):
        return nc.dram_tensor(name, shape, dt).ap()

    g.xT_in = din("xT", [NS, D, S_])
    g.pos_in = din("pos", [NS, 128, S_], I32)
    NQ = c.NH + c.NH // 2
    wf = dict(win=din("win", [L, c.NCH, 128, KC * 128]), wq=din("wq", [L, NQ, 128, c.QC * 128]),
              wk=din("wk", [L, c.NH, 128, c.KVC * 128]), wv=din("wv", [L, 128, c.KVC * D]),
              wo=din("wo", [L, KC, 128, KC * 128]), wp=din("wp", [L, KC, 128, KC * 128]),
              wout=din("wout", [L, KC, 128, KC * 128]))
    vec_in = din("vec", [128, c.NV])
    g.wdw_in = din("wdw", [L, 128, KC * c.CK])
    ident_in = din("ident", [128, 128], BF16)
    perm_in = din("perm", [128, 128])
    masks_in = din("masks", [128, 4 * 512], BF16)
    onesb_in = din("ones_b", [128, 128], BF16)
    onesf_in = din("ones_f", [128, 128])
    zerosb_in = din("zeros_b", [128, 32], BF16)
    g.out_ap = nc.dram_tensor("out", [NS, D, S_], F32, kind="ExternalOutput").ap()

    shp = dict(win=[c.NCH, 128, KC * 128], wq=[NQ, 128, c.QC * 128], wk=[c.NH, 128, c.KVC * 128],
               wv=[128, c.KVC * D], wo=[KC, 128, KC * 128], wp=[KC, 128, KC * 128], wout=[KC, 128, KC * 128])
    g.wb = {k: [dscr(f"{k}_b{l}", v) for l in range(L)] for k, v in shp.items()}

    g.xs = dscr("xs", [D, S_], F32)
    g.cosT = dscr("cosT", [128, S_], F32)
    g.sinT = dscr("sinT", [128, S_], F32)
    g.cq_s = dscr("cq_s", [c.QL, S_])
    g.ckv_s = dscr("ckv_s", [c.KVL, S_])
    g.krr_s = dscr("krr_s", [128, S_])
    g.gm_s = dscr("gm_s", [D, S_])
    g.u_s = dscr("u_s", [D, 32 + S_])
    g.gc_s = dscr("gc_s", [D, S_])
    g.ma_s = dscr("ma_s", [D, S_])
    g.mc_s = dscr("mc_s", [D, S_])
    g.kn_s = dscr("kn_s", [D, S_])
    g.v_s = dscr("v_s", [S_, D])
    g.qn_s = dscr("qn_s", [D, S_])
    g.qr_s = dscr("qr_s", [D // 2, S_])
    g.ya_s = dscr("ya_s", [D, S_])
    g.cv_s = dscr("cv_s", [D, S_])
    g.yc_s = dscr("yc_s", [D, S_])

    def sb(name, shape, dt):
        return es.enter_context(nc.sbuf_tensor(name, shape, dt))
    ps = es.enter_context(nc.psum_tensor("ps", [128, 4096], F32))
    g.PB = [Res() for _ in range(8)]
    g.bank = lambda b, n=1: ps[:, b * 512:(b + n) * 512]

    S.new_sem("D_cst")
    S.new_sem("D_x")
    S.new_sem("D_uz")
    CR = g.CR = Res()
    g.vec = sb("vec_sb", [128, c.NV], F32)
    g.ident = sb("ident_sb", [128, 128], BF16)
    g.perm = sb("perm_sb", [128, 128], F32)
    g.masks = sb("masks_sb", [128, 4, 512], BF16)
    g.onesb = sb("onesb_sb", [128, 128], BF16)
    g.onesf = sb("onesf_sb", [128, 128], F32)
    g.zerosb = sb("zerosb_sb", [128, 32], BF16)
    S.dmas("sp", [(g.vec[:], vec_in), (g.ident[:], ident_in), (g.perm[:], perm_in), (g.onesb[:], onesb_in),
                  (g.onesf[:], onesf_in), (g.zerosb[:], zerosb_in),
                  (g.masks[:], masks_in.rearrange("p (d f) -> p d f", d=4))], "D_cst", w=[CR])

    g.WR = {}
    g.WSTEP = step = max(1, c.NCH // 8)
    g.p1done = [Res() for _ in range(L)]

    def emit_cast(l):
        prev = [g.p1done[l - 1]] if l > 0 else []
        k2 = S.new_sem(f"C_ot{l}")
        r2 = g.WR[("oth", l)] = Res()
        ranges = [(j0, min(c.NCH, j0 + step)) for j0 in range(0, c.NCH, step)]
        if l == 0:
            for pi_, (j0, j1) in enumerate(ranges):
                kk = S.new_sem(f"C_in0_{pi_}")
                rr = g.WR[("win", 0, pi_)] = Res()
                S.dmas("pool", [(g.wb["win"][0][j0:j1], wf["win"][0, j0:j1])], kk, r=prev, w=[rr])
                prev = [rr]
        else:
            kk = S.new_sem(f"C_in{l}")
            rr = Res()
            for pi_ in range(len(ranges)):
                g.WR[("win", l, pi_)] = rr
            S.dmas("pool", [(g.wb["win"][l][j0:j1], wf["win"][l, j0:j1]) for j0, j1 in ranges], kk, r=prev, w=[rr])
        S.dmas("pool", [(g.wb[k][l], wf[k][l]) for k in ("wk", "wv", "wq", "wo", "wp", "wout")], k2, r=prev, w=[r2])
    g.emit_cast = emit_cast
    g.cast_done = set()
    emit_cast(0)
    g.cast_done.add(0)

    BSZ = max(KC * TB1 // 2, 3 * S_, KC * 512, 32 + S_, c.NH * c.KVC * 128, c.KVC * D, max(c.KVC, c.QC) * TB1)
    MSZ = max(2 * S_, 2 * c.CK * 128)
    g.BSZ, g.MSZ = BSZ, MSZ

    def view(ap, name):
        return Slot(ap, S.new_sem("D_" + name))
    B = [sb(f"B{i}", [128, BSZ], BF16) for i in range(3)]
    M0 = sb("M0", [128, MSZ], BF16)
    M1 = sb("M1", [128, S_], BF16)
    g.vB = [view(B[i][:], f"B{i}") for i in range(3)]
    g.vB2lo = view(B[2][:, 0:S_], "B2lo")
    g.vB2hi = view(B[2][:, BSZ // 2:BSZ // 2 + S_], "B2hi") if BSZ // 2 >= S_ else None
    g.vM0 = view(M0[:, 0:S_], "M0")
    g.vM0a = view(M0[:, 0:c.CK * 128], "M0a")
    g.vM0b = view(M0[:, MSZ // 2:MSZ // 2 + c.CK * 128], "M0b")
    g.vM1 = view(M1[:], "M1")
    g.vM0hi = view(M0[:, MSZ // 2:MSZ // 2 + S_], "M0hi")
    g.wsl = Pool_(S, "wsl", 3, [128, KC * 128], BF16)
    g.stg = Pool_(S, "stg", 4, [128, TB1], BF16)
    g.f32a = Pool_(S, "f32a", 3, [128, TB1], F32)
    g.sml = Pool_(S, "sml", 4, [128, 512], F32)
    g.smb = Pool_(S, "smb", 6, [128, 512], BF16)
    g.rsb = view(sb("rsb", [128, TB1], F32)[:], "rsb")
    g.wdb = view(sb("wdb", [128, KC * c.CK], F32)[:], "wdb")
    g.st3 = [view(sb(f"st3_{i}", [128, 512], F32)[:], f"st3_{i}") for i in range(3)]

    def VL(l, off):
        b = c.V_L + l * c.V_PER + off
        return g.vec[:, b:b + 1]
    g.VL = VL

    def store(dst_ap, slot, src_ap, wres, q="act"):
        S.dma(q, dst_ap, src_ap, slot.sem, r=[slot.res], w=wres)

    def load(slot, pairs, rres, split=1):
        out = []
        for d_, s_ in pairs:
            if split == 1:
                out.append((d_, s_))
            else:
                n = d_.shape[1]
                st = n // split
                for i in range(split):
                    out.append((d_[:, i * st:(i + 1) * st], s_[:, i * st:(i + 1) * st]))
        S.dmas("sp", out, slot.sem, r=rres, w=[slot.res])
    g.store, g.load = store, load

    def rstd_from_banks(b0, nb, n_feat, dst):
        d_ = dst.t[:, 0:nb * 512]
        S.op("dve", lambda e: e.tensor_scalar(out=d_, in0=g.bank(b0, nb), scalar1=1.0 / n_feat, scalar2=EPS,
                                              op0=ALU.mult, op1=ALU.add), r=g.PB[b0:b0 + nb], w=[dst.res])
        S.op("act", lambda e: e.activation(out=d_, in_=d_, func=AF.Sqrt), r=[dst.res], w=[dst.res])
        S.op("dve", lambda e: e.reciprocal(out=d_, in_=d_), r=[dst.res], w=[dst.res])
    g.rstd_from_banks = rstd_from_banks
    g.pcount = 0

    def pslot():
        n = 8 // NSB
        b0 = (g.pcount % n) * NSB
        g.pcount += 1
        return b0
    g.pslot = pslot

    for s in range(NS):
        g.X = [[S.R("x", kc, tb) for tb in range(c.NB)] for kc in range(KC)]
        g.xres = lambda kc, t0, t1: [g.X[kc][tb] for tb in range(t0 // 512, t1 // 512)]
        g.xsrc = [g.xT_in[s]] + [g.xs] * (L - 1)
        if s == 0:
            S.dmas("sp", [(g.u_s[kc * 128:(kc + 1) * 128, 0:32], g.zerosb[:]) for kc in range(KC)], "D_uz", r=[CR],
                   w=[S.R("u", kc) for kc in range(KC)])
        for tb in range(c.NTB1):
            t0 = tb * TB1
            pi = g.rsb
            S.dma("sp", pi.t.bitcast(I32), g.pos_in[s, :, t0:t0 + TB1], pi.sem, w=[pi.res])
            A, Bf, Cr = g.f32a.next(), g.f32a.next(), g.f32a.next()
            TWO_PI = 2.0 * math.pi
            C1 = 6.28125
            C2 = TWO_PI - C1

            def dv(fn, r, w):
                S.op("dve", fn, r=r, w=w)
            dv(lambda e: e.tensor_copy(out=A.t, in_=pi.t.bitcast(I32)), [pi.res], [A.res])
            dv(lambda e: e.tensor_scalar(out=A.t, in0=A.t, scalar1=g.vec[:, c.V_INVF:c.V_INVF + 1], scalar2=None,
                                         op0=ALU.mult), [A.res, CR], [A.res])
            dv(lambda e: e.tensor_scalar(out=pi.t.bitcast(I32), in0=A.t, scalar1=1.0 / TWO_PI, scalar2=None,
                                         op0=ALU.mult), [A.res], [pi.res])
            dv(lambda e: e.tensor_copy(out=Bf.t, in_=pi.t.bitcast(I32)), [pi.res], [Bf.res])
            dv(lambda e: e.scalar_tensor_tensor(out=Cr.t, in0=Bf.t, scalar=-C1, in1=A.t, op0=ALU.mult,
                                                op1=ALU.add), [Bf.res, A.res], [Cr.res])
            dv(lambda e: e.scalar_tensor_tensor(out=Cr.t, in0=Bf.t, scalar=-C2, in1=Cr.t, op0=ALU.mult,
                                                op1=ALU.add), [Bf.res, Cr.res], [Cr.res])

            def wrap(X_, M_):
                dv(lambda e: e.tensor_scalar(out=M_.t, in0=X_.t, scalar1=-math.pi, scalar2=None, op0=ALU.is_lt),
                   [X_.res], [M_.res])
                dv(lambda e: e.scalar_tensor_tensor(out=X_.t, in0=M_.t, scalar=TWO_PI, in1=X_.t, op0=ALU.mult,
                                                    op1=ALU.add), [M_.res, X_.res], [X_.res])
                dv(lambda e: e.tensor_scalar(out=M_.t, in0=X_.t, scalar1=math.pi, scalar2=None, op0=ALU.is_gt),
                   [X_.res], [M_.res])
                dv(lambda e: e.scalar_tensor_tensor(out=X_.t, in0=M_.t, scalar=-TWO_PI, in1=X_.t, op0=ALU.mult,
                                                    op1=ALU.add), [M_.res, X_.res], [X_.res])
            wrap(Cr, Bf)
            dv(lambda e: e.tensor_scalar(out=A.t, in0=Cr.t, scalar1=0.5 * math.pi, scalar2=None, op0=ALU.add),
               [Cr.res], [A.res])
            wrap(A, Bf)
            for (dst, ag) in ((g.sinT, Cr), (g.cosT, A)):
                S.op("act", lambda e, ag=ag: e.activation(out=ag.t, in_=ag.t, func=AF.Sin), r=[ag.res], w=[ag.res])
                store(dst[:, t0:t0 + TB1], ag, ag.t, [S.R("rope")])

        for l in range(L):
            layer(g, l)

        for tb in range(c.NTB1):
            t0 = tb * TB1
            rs = norm_stats(g, lambda kc: g.xs[kc * 128:(kc + 1) * 128, t0:t0 + TB1],
                            lambda kc: g.xres(kc, t0, t0 + TB1), KC, D, F32)
            for kc in range(KC):
                xi = g.f32a.next()
                load(xi, [(xi.t, g.xs[kc * 128:(kc + 1) * 128, t0:t0 + TB1])], g.xres(kc, t0, t0 + TB1))
                S.op("dve", lambda e, xi=xi, kc=kc, rs=rs: e.scalar_tensor_tensor(
                    out=xi.t, in0=xi.t, scalar=g.vec[:, c.V_GFIN + kc:c.V_GFIN + kc + 1], in1=rs.t,
                    op0=ALU.mult, op1=ALU.mult), r=[xi.res, rs.res, CR], w=[xi.res])
                store(g.out_ap[s, kc * 128:(kc + 1) * 128, t0:t0 + TB1], xi, xi.t, [S.R("out", kc, tb)])
    S.wait_all("sp", [S.R("out", kc, tb) for kc in range(KC) for tb in range(c.NTB1)])
    S.emit()
    es.close()
    return nc


def norm_stats(g, src_fn, res_fn, nchunk, nfeat, dt):
    c, S = g.c, g.S
    NSB = c.NSB
    for kc in range(nchunk):
        xi = (g.f32a if dt == F32 else g.stg).next()
        g.load(xi, [(xi.t, src_fn(kc))], res_fn(kc))
        sq = g.stg.next()
        S.op("act", lambda e, xi=xi, sq=sq: e.activation(out=sq.t, in_=xi.t, func=AF.Square), r=[xi.res],
             w=[sq.res])

        def f(e, sq=sq, kc=kc):
            for sb_ in range(NSB):
                m = e.matmul(g.bank(sb_), g.onesb[:], sq.t[:, sb_ * 512:(sb_ + 1) * 512], start=(kc == 0),
                             stop=(kc == nchunk - 1))
            return m
        S.op("pe", f, r=[sq.res, g.CR], w=g.PB[0:NSB])
    g.rstd_from_banks(0, NSB, nfeat, g.rsb)
    return g.rsb


def layer(g, l):
    c, S = g.c, g.S
    D, S_, KC, TB1, NSB = c.D, c.S, c.KC, c.TB1, c.NSB
    bank, PB, load, store, CR, VL, WR = g.bank, g.PB, g.load, g.store, g.CR, g.VL, g.WR
    stg, f32a, sml, smb, wsl = g.stg, g.f32a, g.sml, g.smb, g.wsl
    R = S.R
    HALF = KC // 2
    scale = 1.0 / math.sqrt(192.0)
    wb = g.wb
    xsrc = g.xsrc[l]

    def rope(src, t0, dst_ap, dst_res):
        cs = f32a.next()
        load(cs, [(cs.t, g.cosT[:, t0:t0 + TB1])], [R("rope")])
        sn = f32a.next()
        load(sn, [(sn.t, g.sinT[:, t0:t0 + TB1])], [R("rope")])

        def f(e):
            for sb_ in range(NSB):
                m = e.matmul(bank(sb_), g.perm[:], src.t[:, sb_ * 512:(sb_ + 1) * 512], start=True, stop=True)
            return m
        S.op("pe", f, r=[src.res, CR], w=PB[0:NSB])
        S.op("dve", lambda e: e.tensor_tensor(out=sn.t, in0=bank(0, NSB), in1=sn.t, op=ALU.mult),
             r=PB[0:NSB] + [sn.res], w=[sn.res])
        S.op("dve", lambda e: e.tensor_tensor(out=cs.t, in0=src.t, in1=cs.t, op=ALU.mult), r=[src.res, cs.res],
             w=[cs.res])
        o = stg.next()
        S.op("dve", lambda e: e.tensor_tensor(out=o.t, in0=cs.t, in1=sn.t, op=ALU.add), r=[cs.res, sn.res],
             w=[o.res])
        store(dst_ap, o, o.t, dst_res)

    hA, hB = g.vB[0], g.vB[1]

    def hT(kc, a, b):
        sl = hA if kc < HALF else hB
        o = (kc % HALF) * TB1
        return sl.t[:, o + a:o + b]
    for tb in range(c.NTB1):
        t0 = tb * TB1
        rs = norm_stats(g, lambda kc: xsrc[kc * 128:(kc + 1) * 128, t0:t0 + TB1],
                        lambda kc: g.xres(kc, t0, t0 + TB1), KC, D, F32)
        for kc in range(KC):
            xi = f32a.next()
            load(xi, [(xi.t, xsrc[kc * 128:(kc + 1) * 128, t0:t0 + TB1])], g.xres(kc, t0, t0 + TB1))
            S.op("dve", lambda e, xi=xi, kc=kc: e.scalar_tensor_tensor(
                out=hT(kc, 0, TB1), in0=xi.t, scalar=VL(l, c.V_GPRE + kc), in1=rs.t, op0=ALU.mult,
                op1=ALU.mult), r=[xi.res, rs.res, CR], w=[(hA if kc < HALF else hB).res])
        uval = None
        for j, (kind, idx, off) in enumerate(c.chunks):
            ws = wsl.next()
            load(ws, [(ws.t, wb["win"][l][j])], [WR[("win", l, j // g.WSTEP)]])
            b0 = g.pslot()

            def f(e, ws=ws, b0=b0):
                for kc in range(KC):
                    for sb_ in range(NSB):
                        m = e.matmul(bank(b0 + sb_), ws.t[:, kc * 128:(kc + 1) * 128],
                                     hT(kc, sb_ * 512, (sb_ + 1) * 512), start=(kc == 0), stop=(kc == KC - 1))
                return m
            last_ = (tb == c.NTB1 - 1 and j == c.NCH - 1)
            S.op("pe", f, r=[ws.res, hA.res, hB.res], w=PB[b0:b0 + NSB] + ([g.p1done[l]] if last_ else []))
            src = bank(b0, NSB)
            pr = PB[b0:b0 + NSB]
            rows = slice(idx * 128, (idx + 1) * 128)
            if kind == "kr":
                kf = f32a.next()
                S.op("act", lambda e, kf=kf, src=src: e.activation(out=kf.t, in_=src, func=AF.Copy), r=pr,
                     w=[kf.res])
                rope(kf, t0, g.krr_s[:, t0:t0 + TB1], [R("krr")])
                continue
            func = {"cq": AF.Copy, "ckv": AF.Copy, "gm": AF.Silu, "uv": AF.Copy, "ug": AF.Sigmoid, "gc": AF.Silu,
                    "ma": AF.Sigmoid, "mc": AF.Sigmoid}[kind]
            o = stg.next()
            S.op("act", lambda e, o=o, src=src, func=func: e.activation(out=o.t, in_=src, func=func), r=pr,
                 w=[o.res])
            if kind == "uv":
                uval = o
                continue
            if kind == "ug":
                S.op("dve", lambda e, o=o, uv=uval: e.tensor_tensor(out=o.t, in0=uv.t, in1=o.t, op=ALU.mult),
                     r=[uval.res, o.res], w=[o.res])
                store(g.u_s[rows, 32 + t0:32 + t0 + TB1], o, o.t, [R("u", idx)])
                continue
            dst, rk = {"cq": (g.cq_s, ("cq", idx)), "ckv": (g.ckv_s, ("ckv", idx)), "gm": (g.gm_s, ("gm", idx)),
                       "gc": (g.gc_s, ("gc", idx)), "ma": (g.ma_s, ("ma", idx)), "mc": (g.mc_s, ("mc", idx))}[kind]
            store(dst[rows, t0:t0 + TB1], o, o.t, [R(*rk)])

    if l + 1 < c.DEPTH and (l + 1) not in g.cast_done:
        g.emit_cast(l + 1)
        g.cast_done.add(l + 1)

    wkS, wvS, lat = g.vB[0], g.vB[1], g.vB[2]
    load(wkS, [(wkS.t[:, 0:c.NH * c.KVC * 128].rearrange("p (h x) -> p h x", h=c.NH),
                wb["wk"][l].rearrange("h p x -> p h x"))], [WR[("oth", l)]], split=max(1, c.NH // 8))
    load(wvS, [(wvS.t[:, 0:c.KVC * D], wb["wv"][l])], [WR[("oth", l)]])
    VG = min(NSB, D // 512)
    for tb in range(c.NTB1):
        t0 = tb * TB1
        rs = norm_stats(g, lambda kc: g.ckv_s[kc * 128:(kc + 1) * 128, t0:t0 + TB1], lambda kc: [R("ckv", kc)], c.KVC,
                        c.KVL, BF16)
        for kc in range(c.KVC):
            xi = stg.next()
            load(xi, [(xi.t, g.ckv_s[kc * 128:(kc + 1) * 128, t0:t0 + TB1])], [R("ckv", kc)])
            S.op("dve", lambda e, xi=xi, kc=kc: e.scalar_tensor_tensor(
                out=lat.t[:, kc * TB1:(kc + 1) * TB1], in0=xi.t, scalar=VL(l, c.V_GKV + kc), in1=rs.t,
                op0=ALU.mult, op1=ALU.mult), r=[xi.res, rs.res, CR], w=[lat.res])
        for h in range(c.NH):
            b0 = g.pslot()

            def f(e, h=h, b0=b0):
                for kc in range(c.KVC):
                    for sb_ in range(NSB):
                        o_ = (h * c.KVC + kc) * 128
                        m = e.matmul(bank(b0 + sb_), wkS.t[:, o_:o_ + 128],
                                     lat.t[:, kc * TB1 + sb_ * 512:kc * TB1 + (sb_ + 1) * 512], start=(kc == 0),
                                     stop=(kc == c.KVC - 1))
                return m
            S.op("pe", f, r=[wkS.res, lat.res], w=PB[b0:b0 + NSB])
            o = stg.next()
            S.op("act", lambda e, o=o, b0=b0: e.activation(out=o.t, in_=bank(b0, NSB), func=AF.Copy),
                 r=PB[b0:b0 + NSB], w=[o.res])
            store(g.kn_s[h * 128:(h + 1) * 128, t0:t0 + TB1], o, o.t, [R("kn", h)])
        for tt in range(TB1 // 128):
            for hg in range(D // (512 * VG)):
                b0 = g.pslot()

                def f(e, tt=tt, hg=hg, b0=b0):
                    for q4 in range(VG):
                        for kc in range(c.KVC):
                            o_ = kc * D + (hg * VG + q4) * 512
                            m = e.matmul(bank(b0 + q4), lat.t[:, kc * TB1 + tt * 128:kc * TB1 + (tt + 1) * 128],
                                         wvS.t[:, o_:o_ + 512], start=(kc == 0), stop=(kc == c.KVC - 1))
                    return m
                S.op("pe", f, r=[wvS.res, lat.res], w=PB[b0:b0 + VG])
                o = stg.next()
                S.op("act", lambda e, o=o, b0=b0: e.activation(out=o.t[:, 0:VG * 512], in_=bank(b0, VG),
                                                               func=AF.Copy), r=PB[b0:b0 + VG], w=[o.res])
                store(g.v_s[t0 + tt * 128:t0 + (tt + 1) * 128, hg * VG * 512:(hg + 1) * VG * 512], o,
                      o.t[:, 0:VG * 512], [R("v", t0 // 128 + tt, hg)])
        rs = norm_stats(g, lambda kc: g.cq_s[kc * 128:(kc + 1) * 128, t0:t0 + TB1], lambda kc: [R("cq", kc)], c.QC,
                        c.QL, BF16)
        for kc in range(c.QC):
            xi = stg.next()
            load(xi, [(xi.t, g.cq_s[kc * 128:(kc + 1) * 128, t0:t0 + TB1])], [R("cq", kc)])
            S.op("dve", lambda e, xi=xi, kc=kc: e.scalar_tensor_tensor(
                out=lat.t[:, kc * TB1:(kc + 1) * TB1], in0=xi.t, scalar=VL(l, c.V_GQ + kc), in1=rs.t,
                op0=ALU.mult, op1=ALU.mult), r=[xi.res, rs.res, CR], w=[lat.res])
        for j in range(c.NH + c.NH // 2):
            ws = wsl.next()
            load(ws, [(ws.t[:, 0:c.QC * 128], wb["wq"][l][j])], [WR[("oth", l)]])
            b0 = g.pslot()

            def f(e, ws=ws, b0=b0):
                for kc in range(c.QC):
                    for sb_ in range(NSB):
                        m = e.matmul(bank(b0 + sb_), ws.t[:, kc * 128:(kc + 1) * 128],
                                     lat.t[:, kc * TB1 + sb_ * 512:kc * TB1 + (sb_ + 1) * 512], start=(kc == 0),
                                     stop=(kc == c.QC - 1))
                return m
            S.op("pe", f, r=[ws.res, lat.res], w=PB[b0:b0 + NSB])
            if j < c.NH:
                o = stg.next()
                S.op("act", lambda e, o=o, b0=b0: e.activation(out=o.t, in_=bank(b0, NSB), func=AF.Copy),
                     r=PB[b0:b0 + NSB], w=[o.res])
                store(g.qn_s[j * 128:(j + 1) * 128, t0:t0 + TB1], o, o.t, [R("qn", j)])
            else:
                qf = f32a.next()
                S.op("act", lambda e, qf=qf, b0=b0: e.activation(out=qf.t, in_=bank(b0, NSB), func=AF.Copy),
                     r=PB[b0:b0 + NSB], w=[qf.res])
                jj = j - c.NH
                rope(qf, t0, g.qr_s[jj * 128:(jj + 1) * 128, t0:t0 + TB1], [R("qr", jj)])

    NKT = S_ // 128
    krS = g.vM0
    inherit(krS, [g.vM0a, g.vM0b])
    inherit(g.vM0hi, [g.vM0a, g.vM0b])
    knv = [g.vB2lo] + ([g.vB2hi] if g.vB2hi is not None else [])
    for v_ in knv:
        inherit(v_, [g.vB[2]])
    qrv = [g.vM1, g.vM0hi]
    load(krS, [(krS.t, g.krr_s[:, :])], [R("krr")])

    def p3_loads(h):
        knS = knv[h % len(knv)]
        load(knS, [(knS.t, g.kn_s[h * 128:(h + 1) * 128, :])], [R("kn", h)])
        hb = g.vB[h % 2]
        vsp = max(1, NKT // 8)
        vst = NKT // vsp
        vd = hb.t[:, S_:2 * S_].rearrange("p (k d) -> p k d", d=128)
        vs = g.v_s[:, h * 128:(h + 1) * 128].rearrange("(k p) d -> p k d", p=128)
        load(hb, [(hb.t[:, 0:S_], g.qn_s[h * 128:(h + 1) * 128, :]),
                  (hb.t[:, 2 * S_:3 * S_], g.gm_s[h * 128:(h + 1) * 128, :])] +
             [(vd[:, i * vst:(i + 1) * vst], vs[:, i * vst:(i + 1) * vst]) for i in range(vsp)],
             [R("qn", h), R("gm", h)] + [R("v", i, (h * 128) // (VG * 512)) for i in range(NKT)])
        if h % 2 == 0:
            qr_ = qrv[(h // 2) % 2]
            load(qr_, [(qr_.t, g.qr_s[(h // 2) * 128:(h // 2 + 1) * 128, :])], [R("qr", h // 2)])
    p3_loads(0)
    for h in range(c.NH):
        if h + 1 < c.NH:
            p3_loads(h + 1)
        knS = knv[h % len(knv)]
        hb = g.vB[h % 2]
        qrS = qrv[(h // 2) % 2]
        pb = (h % 2) * 64
        for qb in range(c.NB):
            nkt = 4 * qb + 4
            OTb = 4 + (qb % 2)
            SUMb = 6 + (qb % 2)
            q0 = qb * 512
            Pt = [None] * nkt

            def st_group(kt, qb=qb, q0=q0, Pt=Pt, knS=knS, hb=hb, pb=pb, qrS=qrS):
                bk = kt % 3

                def f(e):
                    e.matmul(bank(bk), knS.t[:, kt * 128:(kt + 1) * 128], hb.t[:, q0:q0 + 512], start=True,
                             stop=False)
                    return e.matmul(bank(bk), krS.t[pb:pb + 64, kt * 128:(kt + 1) * 128],
                                    qrS.t[pb:pb + 64, q0:q0 + 512], start=False, stop=True)
                S.op("pe", f, r=[knS.res, hb.res, krS.res, qrS.res], w=[PB[bk]])
                p = smb.next()
                Pt[kt] = p
                S.op("act", lambda e: e.activation(out=p.t, in_=bank(bk), func=AF.Exp, scale=scale), r=[PB[bk]],
                     w=[p.res])
                d = kt - 4 * qb
                if d >= 0:
                    S.op("pool", lambda e: e.tensor_tensor(out=p.t, in0=p.t, in1=g.masks[:, d, :], op=ALU.mult),
                         r=[p.res, CR], w=[p.res])

            def pv_group(kt, nkt=nkt, OTb=OTb, SUMb=SUMb, Pt=Pt, hb=hb):
                p = Pt[kt]

                def f(e):
                    e.matmul(bank(OTb), hb.t[:, S_ + kt * 128:S_ + (kt + 1) * 128], p.t, start=(kt == 0),
                             stop=(kt == nkt - 1))
                    return e.matmul(bank(SUMb), g.onesb[:], p.t, start=(kt == 0), stop=(kt == nkt - 1))
                S.op("pe", f, r=[hb.res, p.res, CR], w=[PB[OTb], PB[SUMb]])
            SK = 2
            for i in range(nkt + SK):
                if i < nkt:
                    st_group(i)
                if i >= SK:
                    pv_group(i - SK)
            ri = sml.next()
            S.op("dve", lambda e, ri=ri, SUMb=SUMb: e.reciprocal(out=ri.t, in_=bank(SUMb)), r=[PB[SUMb]],
                 w=[ri.res])
            S.op("dve", lambda e, ri=ri, OTb=OTb: e.tensor_tensor(out=ri.t, in0=bank(OTb), in1=ri.t, op=ALU.mult),
                 r=[PB[OTb], ri.res], w=[ri.res])
            yo = smb.next()
            S.op("dve", lambda e, ri=ri, yo=yo, hb=hb, q0=q0: e.tensor_tensor(
                out=yo.t, in0=ri.t, in1=hb.t[:, 2 * S_ + q0:2 * S_ + q0 + 512], op=ALU.mult), r=[ri.res, hb.res],
                w=[yo.res])
            store(g.ya_s[h * 128:(h + 1) * 128, q0:q0 + 512], yo, yo.t, [R("ya", qb)], q="sp")

    wd = g.wdb
    load(wd, [(wd.t, g.wdw_in[l])], [])
    inherit(g.vM0a, [g.vM0])
    inherit(g.vM0b, [g.vM0, g.vM0hi])
    CV2 = min(2, c.NB)
    for cc in range(KC):
        dg = g.vM0a if cc % 2 == 0 else g.vM0b

        def fd(e, cc=cc, dg=dg):
            for k in range(c.CK):
                m = e.tensor_scalar(out=dg.t[:, k * 128:(k + 1) * 128], in0=g.ident[:],
                                    scalar1=wd.t[:, cc * c.CK + k:cc * c.CK + k + 1], scalar2=None, op0=ALU.mult)
            return m
        S.op("dve", fd, r=[wd.res, CR], w=[dg.res])
        uu = g.vB[cc % 2]
        load(uu, [(uu.t[:, 0:32 + S_], g.u_s[cc * 128:(cc + 1) * 128, :])], [R("u", cc)])
        for t4 in range(0, c.NB, CV2):
            b0 = g.pslot()

            def f(e, t4=t4, b0=b0, dg=dg, uu=uu):
                for q in range(CV2):
                    ts = (t4 + q) * 512
                    for k in range(c.CK):
                        a = 32 + ts - (c.CK - 1) + k
                        m = e.matmul(bank(b0 + q), dg.t[:, k * 128:(k + 1) * 128], uu.t[:, a:a + 512],
                                     start=(k == 0), stop=(k == c.CK - 1))
                return m
            S.op("pe", f, r=[dg.res, uu.res], w=PB[b0:b0 + CV2])
            o = stg.next()
            S.op("act", lambda e, o=o, b0=b0, cc=cc: e.activation(
                out=o.t[:, 0:CV2 * 512], in_=bank(b0, CV2), func=AF.Identity, bias=VL(l, c.V_BDW + cc)),
                r=PB[b0:b0 + CV2] + [CR], w=[o.res])
            store(g.cv_s[cc * 128:(cc + 1) * 128, t4 * 512:(t4 + CV2) * 512], o, o.t[:, 0:CV2 * 512], [R("cv", cc, t4)])

    inherit(g.vB[2], knv)
    mean, rstd, nmr = g.st3
    for tbk in range(c.NB):
        q0 = tbk * 512
        cvS = g.vB[0] if tbk % 2 == 0 else g.vB[2]
        load(cvS, [(cvS.t[:, 0:KC * 512].rearrange("p (c t) -> p c t", t=512),
                    g.cv_s[:, q0:q0 + 512].rearrange("(c p) t -> p c t", p=128))],
             [R("cv", i, (tbk // CV2) * CV2) for i in range(KC)], split=max(1, KC // 8))
        gcS = g.vB[1]
        load(gcS, [(gcS.t[:, 0:KC * 512].rearrange("p (c t) -> p c t", t=512),
                    g.gc_s[:, q0:q0 + 512].rearrange("(c p) t -> p c t", p=128))], [R("gc", i) for i in range(KC)],
             split=max(1, KC // 8))
        for cc in range(KC):
            sq = smb.next()
            S.op("act", lambda e, cc=cc, sq=sq, cvS=cvS: e.activation(
                out=sq.t, in_=cvS.t[:, cc * 512:(cc + 1) * 512], func=AF.Square), r=[cvS.res], w=[sq.res])

            def f(e, cc=cc, sq=sq, cvS=cvS):
                e.matmul(bank(0), g.onesb[:], cvS.t[:, cc * 512:(cc + 1) * 512], start=(cc == 0),
                         stop=(cc == KC - 1))
                return e.matmul(bank(1), g.onesb[:], sq.t, start=(cc == 0), stop=(cc == KC - 1))
            S.op("pe", f, r=[cvS.res, sq.res, CR], w=PB[0:2])

        S.op("dve", lambda e: e.tensor_scalar(out=mean.t, in0=bank(0), scalar1=1.0 / D, scalar2=None, op0=ALU.mult),
             r=[PB[0]], w=[mean.res])
        S.op("dve", lambda e: e.tensor_tensor(out=nmr.t, in0=mean.t, in1=mean.t, op=ALU.mult), r=[mean.res],
             w=[nmr.res])
        S.op("dve", lambda e: e.scalar_tensor_tensor(out=rstd.t, in0=bank(1), scalar=1.0 / D, in1=nmr.t,
                                                     op0=ALU.mult, op1=ALU.subtract), r=[PB[1], nmr.res],
             w=[rstd.res])
        S.op("dve", lambda e: e.tensor_scalar(out=rstd.t, in0=rstd.t, scalar1=EPS, scalar2=None, op0=ALU.add),
             r=[rstd.res], w=[rstd.res])
        S.op("act", lambda e: e.activation(out=rstd.t, in_=rstd.t, func=AF.Sqrt), r=[rstd.res], w=[rstd.res])
        S.op("dve", lambda e: e.reciprocal(out=rstd.t, in_=rstd.t), r=[rstd.res], w=[rstd.res])
        S.op("dve", lambda e: e.scalar_tensor_tensor(out=nmr.t, in0=mean.t, scalar=-1.0, in1=rstd.t, op0=ALU.mult,
                                                     op1=ALU.mult), r=[mean.res, rstd.res], w=[nmr.res])
        for cc in range(KC):
            t1 = sml.next()

            S.op("dve", lambda e, cc=cc, t1=t1, cvS=cvS: e.tensor_tensor(
                out=t1.t, in0=cvS.t[:, cc * 512:(cc + 1) * 512], in1=rstd.t, op=ALU.mult), r=[cvS.res, rstd.res],
                w=[t1.res])
            S.op("dve", lambda e, t1=t1: e.tensor_tensor(out=t1.t, in0=t1.t, in1=nmr.t, op=ALU.add),
                 r=[t1.res, nmr.res], w=[t1.res])
            S.op("act", lambda e, cc=cc, t1=t1: e.activation(out=t1.t, in_=t1.t, func=AF.Silu,
                                                             scale=VL(l, c.V_GCN + cc), bias=VL(l, c.V_BCN + cc)),
                 r=[t1.res, CR], w=[t1.res])
            yo = smb.next()
            S.op("pool", lambda e, cc=cc, t1=t1, yo=yo: e.tensor_tensor(
                out=yo.t, in0=t1.t, in1=gcS.t[:, cc * 512:(cc + 1) * 512], op=ALU.mult), r=[t1.res, gcS.res],
                w=[yo.res])
            store(g.yc_s[cc * 128:(cc + 1) * 128, q0:q0 + 512], yo, yo.t, [R("yc", tbk)], q="pool")

    yaS, ycS, yS = g.vB[0], g.vB[1], g.vB[2]
    for tbk in range(c.NB):
        q0 = tbk * 512
        load(yaS, [(yaS.t[:, 0:KC * 512].rearrange("p (c t) -> p c t", t=512),
                    g.ya_s[:, q0:q0 + 512].rearrange("(c p) t -> p c t", p=128))], [R("ya", tbk)],
             split=max(1, KC // 8))
        load(ycS, [(ycS.t[:, 0:KC * 512].rearrange("p (c t) -> p c t", t=512),
                    g.yc_s[:, q0:q0 + 512].rearrange("(c p) t -> p c t", p=128))], [R("yc", tbk)],
             split=max(1, KC // 8))
        for m_ in range(KC):
            for src_, wkey, bk in ((yaS, "wo", 0), (ycS, "wp", 1)):
                ws = wsl.next()
                load(ws, [(ws.t, wb[wkey][l][m_])], [WR[("oth", l)]])

                def f(e, ws=ws, src_=src_, bk=bk):
                    for kc in range(KC):
                        m = e.matmul(bank(bk), ws.t[:, kc * 128:(kc + 1) * 128], src_.t[:, kc * 512:(kc + 1) * 512],
                                     start=(kc == 0), stop=(kc == KC - 1))
                    return m
                S.op("pe", f, r=[ws.res, src_.res], w=[PB[bk]])
            gA, gC = smb.next(), smb.next()
            load(gA, [(gA.t, g.ma_s[m_ * 128:(m_ + 1) * 128, q0:q0 + 512])], [R("ma", m_)])
            load(gC, [(gC.t, g.mc_s[m_ * 128:(m_ + 1) * 128, q0:q0 + 512])], [R("mc", m_)])
            t1, t2 = sml.next(), sml.next()
            S.op("dve", lambda e, t1=t1, gA=gA: e.tensor_tensor(out=t1.t, in0=bank(0), in1=gA.t, op=ALU.mult),
                 r=[PB[0], gA.res], w=[t1.res])
            S.op("dve", lambda e, t2=t2, gC=gC: e.tensor_tensor(out=t2.t, in0=bank(1), in1=gC.t, op=ALU.mult),
                 r=[PB[1], gC.res], w=[t2.res])
            S.op("dve", lambda e, t1=t1, t2=t2, m_=m_: e.tensor_tensor(out=yS.t[:, m_ * 512:(m_ + 1) * 512],
                                                                       in0=t1.t, in1=t2.t, op=ALU.add),
                 r=[t1.res, t2.res], w=[yS.res])
        for m_ in range(KC):
            ws = wsl.next()
            load(ws, [(ws.t, wb["wout"][l][m_])], [WR[("oth", l)]])
            bk = 2 + (m_ % 2)

            def f(e, ws=ws, bk=bk):
                for kc in range(KC):
                    m = e.matmul(bank(bk), ws.t[:, kc * 128:(kc + 1) * 128], yS.t[:, kc * 512:(kc + 1) * 512],
                                 start=(kc == 0), stop=(kc == KC - 1))
                return m
            S.op("pe", f, r=[ws.res, yS.res], w=[PB[bk]])
            xi = sml.next()
            load(xi, [(xi.t, xsrc[m_ * 128:(m_ + 1) * 128, q0:q0 + 512])], [g.X[m_][tbk]])
            S.op("dve", lambda e, xi=xi, bk=bk: e.tensor_tensor(out=xi.t, in0=bank(bk), in1=xi.t, op=ALU.add),
                 r=[PB[bk], xi.res], w=[xi.res])
            store(g.xs[m_ * 128:(m_ + 1) * 128, q0:q0 + 512], xi, xi.t, [g.X[m_][tbk]])


def run(cfg, inputs, trace=False):
    per = host_prep(cfg, inputs)
    nc = build_program(cfg)
    res = run_bass_kernel_spmd(nc, per, core_ids=list(range(cfg.ncores)), trace=trace)
    outs = []
    for core in range(cfg.ncores):
        o = res.results[core]["out"]
        for i in range(cfg.NS):
            outs.append(np.ascontiguousarray(o[i].T))
    out = np.stack(outs).astype(np.float32)
    return (out, res) if trace else out


def kernel(**inputs):
    return run(Cfg(), inputs)
```

```python
import math
import types
from contextlib import ExitStack
import numpy as np
import ml_dtypes
import concourse.bass as bass
import concourse.mybir as mybir
from concourse.bass_utils import run_bass_kernel_spmd

F32 = mybir.dt.float32
BF16 = mybir.dt.bfloat16
I32 = mybir.dt.int32
AF = mybir.ActivationFunctionType
ALU = mybir.AluOpType
EPS = 1e-6
NCORES = 4


class Cfg:
    def __init__(self, D=4096, S=4096, NH=32, QL=1024, KVL=512, DEPTH=4, CK=31, BATCH=4, ncores=NCORES):
        self.D, self.S, self.NH, self.QL, self.KVL, self.DEPTH, self.CK = D, S, NH, QL, KVL, DEPTH, CK
        self.BATCH = BATCH
        self.ncores = ncores
        self.NS = BATCH // ncores
        self.KC = D // 128
        self.QC = QL // 128
        self.KVC = KVL // 128
        self.TB1 = min(1024, S)
        self.NSB = self.TB1 // 512
        self.NTB1 = S // self.TB1
        self.NB = S // 512
        assert NH * 128 == D
        self.OFF_CQ = 0
        self.OFF_CKV = QL
        self.OFF_KR = QL + KVL
        self.OFF_GM = self.OFF_KR + 64
        self.OFF_CV = self.OFF_GM + D
        self.OFF_GC = self.OFF_CV + 2 * D
        self.OFF_MA = self.OFF_GC + D
        self.OFF_MC = self.OFF_MA + D
        self.INW = self.OFF_MC + D
        ch = []
        for i in range(self.QC):
            ch.append(("cq", i, self.OFF_CQ + 128 * i))
        for i in range(self.KVC):
            ch.append(("ckv", i, self.OFF_CKV + 128 * i))
        ch.append(("kr", 0, self.OFF_KR))
        for i in range(self.KC):
            ch.append(("gm", i, self.OFF_GM + 128 * i))
        for i in range(self.KC):
            ch.append(("uv", i, self.OFF_CV + 128 * i))
            ch.append(("ug", i, self.OFF_CV + D + 128 * i))
        for i in range(self.KC):
            ch.append(("gc", i, self.OFF_GC + 128 * i))
        for i in range(self.KC):
            ch.append(("ma", i, self.OFF_MA + 128 * i))
        for i in range(self.KC):
            ch.append(("mc", i, self.OFF_MC + 128 * i))
        self.chunks = ch
        self.NCH = len(ch)
        o = 0
        self.V_INVF = o; o += 1
        self.V_GFIN = o; o += self.KC
        self.V_L = o
        self.V_GPRE = 0
        self.V_GQ = self.KC
        self.V_GKV = self.V_GQ + self.QC
        self.V_BDW = self.V_GKV + self.KVC
        self.V_GCN = self.V_BDW + self.KC
        self.V_BCN = self.V_GCN + self.KC
        self.V_PER = self.V_BCN + self.KC
        self.NV = self.V_L + DEPTH * self.V_PER


def _chunked(w, cols_list):
    K = w.shape[0]
    out = np.empty((len(cols_list), 128, K // 128, 128), np.float32)
    for j, cols in enumerate(cols_list):
        blk = w[:, cols]
        out[j] = blk.reshape(K // 128, 128, 128).transpose(1, 0, 2)
    return out


def host_consts(cfg):
    ident = np.eye(128, dtype=np.float32).astype(ml_dtypes.bfloat16)
    perm = np.zeros((128, 128), np.float32)
    for b in (0, 64):
        for i in range(32):
            perm[b + i + 32, b + i] = -1.0
            perm[b + i, b + i + 32] = 1.0
    masks = np.zeros((128, 4, 512), np.float32)
    p = np.arange(128)[:, None]
    f = np.arange(512)[None, :]
    for d in range(4):
        masks[:, d, :] = (d * 128 + p <= f)
    masks = masks.astype(ml_dtypes.bfloat16)
    ones_b = np.ones((128, 128), np.float32).astype(ml_dtypes.bfloat16)
    ones_f = np.ones((128, 128), np.float32)
    zeros_b = np.zeros((128, 32), np.float32).astype(ml_dtypes.bfloat16)
    return dict(ident=ident, perm=perm, masks=masks, ones_b=ones_b, ones_f=ones_f, zeros_b=zeros_b)


def host_prep(cfg, inp):
    c = cfg
    L = c.DEPTH
    ar = np.arange(128)
    sh = {}
    cols_in = []
    for kind, i, off in c.chunks:
        if kind == "kr":
            cols_in.append(off + (ar % 64))
        else:
            cols_in.append(off + ar)
    sh["win"] = np.stack([_chunked(np.asarray(inp["w_in"][l]), cols_in) for l in range(L)])
    cols_q = [h * 192 + ar for h in range(c.NH)]
    for j in range(c.NH // 2):
        cols_q.append(np.where(ar < 64, (2 * j) * 192 + 128 + ar, (2 * j + 1) * 192 + 128 + (ar - 64)))
    sh["wq"] = np.stack([_chunked(np.asarray(inp["w_q_up"][l]), cols_q) for l in range(L)])
    cols_k = [h * 256 + ar for h in range(c.NH)]
    sh["wk"] = np.stack([_chunked(np.asarray(inp["w_kv_up"][l]), cols_k) for l in range(L)])
    vcols = np.concatenate([h * 256 + 128 + ar for h in range(c.NH)])
    wv = np.stack([np.asarray(inp["w_kv_up"][l])[:, vcols] for l in range(L)])
    sh["wv"] = np.ascontiguousarray(wv.reshape(L, c.KVC, 128, c.D).transpose(0, 2, 1, 3))
    std = [ar + 128 * i for i in range(c.KC)]
    sh["wo"] = np.stack([_chunked(np.asarray(inp["w_o_mla"][l]), std) for l in range(L)])
    sh["wp"] = np.stack([_chunked(np.asarray(inp["w_pw_out"][l]), std) for l in range(L)])
    sh["wout"] = np.stack([_chunked(np.asarray(inp["w_out"][l]), std) for l in range(L)])
    vec = np.zeros((128, c.NV), np.float32)
    invf = (1.0 / (10000.0 ** (np.arange(0, 64, 2, dtype=np.float32) / np.float32(64.0)))).astype(np.float32)
    vec[:, c.V_INVF] = invf[ar % 32]

    def fm(v):
        return np.asarray(v, np.float32).reshape(-1, 128).T

    vec[:, c.V_GFIN:c.V_GFIN + c.KC] = fm(inp["g_final"])
    for l in range(L):
        b = c.V_L + l * c.V_PER
        vec[:, b + c.V_GPRE:b + c.V_GPRE + c.KC] = fm(inp["g_pre"][l])
        vec[:, b + c.V_GQ:b + c.V_GQ + c.QC] = fm(inp["g_q"][l])
        vec[:, b + c.V_GKV:b + c.V_GKV + c.KVC] = fm(inp["g_kv"][l])
        vec[:, b + c.V_BDW:b + c.V_BDW + c.KC] = fm(inp["b_dw"][l])
        vec[:, b + c.V_GCN:b + c.V_GCN + c.KC] = fm(inp["g_cn"][l])
        vec[:, b + c.V_BCN:b + c.V_BCN + c.KC] = fm(inp["b_cn"][l])
    sh["vec"] = vec
    wdw = np.asarray(inp["w_dw"], np.float32)
    sh["wdw"] = np.ascontiguousarray(wdw.reshape(L, c.CK, c.KC, 128).transpose(0, 3, 2, 1))
    sh.update(host_consts(c))
    for k in ("win", "wq", "wk", "wo", "wp", "wout"):
        a = sh[k]
        sh[k] = a.reshape(a.shape[0], a.shape[1], 128, a.shape[3] * 128)
    sh["wv"] = sh["wv"].reshape(L, 128, c.KVC * c.D)
    sh["wdw"] = sh["wdw"].reshape(L, 128, c.KC * c.CK)
    sh["masks"] = sh["masks"].reshape(128, 4 * 512)
    x = np.asarray(inp["x"], np.float32)
    pos = np.asarray(inp["positions"], np.int32)
    per = []
    for core in range(c.ncores):
        seqs = range(core * c.NS, (core + 1) * c.NS)
        xT = np.stack([np.ascontiguousarray(x[b].T) for b in seqs])
        pr = np.stack([np.broadcast_to(pos[b][None, :], (128, c.S)).copy() for b in seqs])
        d = dict(sh)
        d["xT"] = xT
        d["pos"] = pr
        per.append(d)
    return per


class Res:
    __slots__ = ("w", "r")

    def __init__(self):
        self.w = None
        self.r = {}


class Slot:
    def __init__(self, t, sem_key):
        self.t = t
        self.sem = sem_key
        self.res = Res()


class Sched:
    ENGS = ("sp", "act", "pool", "dve", "pe")

    def __init__(self, nc, es):
        self.nc = nc
        self.es = es
        self.ops = {e: [] for e in self.ENGS}
        self.cnt = {}
        self.sems = {}
        self.waited = {e: {} for e in self.ENGS}
        for e in self.ENGS:
            self.new_sem("E_" + e)
        self.dres = {}

    def new_sem(self, key):
        self.sems[key] = self.es.enter_context(self.nc.semaphore(key))
        self.cnt[key] = 0
        return key

    def R(self, *key):
        r = self.dres.get(key)
        if r is None:
            r = self.dres[key] = Res()
        return r

    def _waits(self, eng, reads, writes):
        need = {}

        def add(tok):
            if tok is None:
                return
            k, v = tok
            if need.get(k, 0) < v:
                need[k] = v
        for r in reads:
            add(r.w)
        for w in writes:
            add(w.w)
            for k, v in w.r.items():
                add((k, v))
        out = []
        wd = self.waited[eng]
        for k, v in need.items():
            if eng == "pe" and k == "E_pe":
                continue
            if wd.get(k, 0) >= v:
                continue
            wd[k] = v
            out.append((k, v))
        return out

    def _commit(self, tok, reads, writes):
        k, v = tok
        for r in reads:
            if r.r.get(k, 0) < v:
                r.r[k] = v
        for w in writes:
            w.w = tok
            w.r = {}

    @staticmethod
    def _freeze(fn):
        if fn is None or not getattr(fn, "__closure__", None):
            return fn
        cells = []
        for c in fn.__closure__:
            try:
                cells.append(types.CellType(c.cell_contents))
            except ValueError:
                cells.append(c)
        nf = types.FunctionType(fn.__code__, fn.__globals__, fn.__name__, fn.__defaults__, tuple(cells))
        nf.__kwdefaults__ = fn.__kwdefaults__
        return nf

    def op(self, eng, fn, r=(), w=()):
        fn = self._freeze(fn)
        waits = self._waits(eng, r, w)
        k = "E_" + eng
        self.cnt[k] += 1
        tok = (k, self.cnt[k])
        self.ops[eng].append((waits, fn, k, 1))
        self._commit(tok, r, w)
        return tok

    def dma(self, q, out_ap, in_ap, semkey, r=(), w=()):
        return self.dmas(q, [(out_ap, in_ap)], semkey, r=r, w=w)

    def dmas(self, q, pairs, semkey, r=(), w=()):
        waits = self._waits(q, r, w)
        for i, (o, i_) in enumerate(pairs):
            self.cnt[semkey] += 16
            self.ops[q].append((waits if i == 0 else [], (lambda e, o=o, i_=i_: e.dma_start(out=o, in_=i_)),
                                semkey, 16))
        tok = (semkey, self.cnt[semkey])
        self._commit(tok, r, w)
        return tok

    def retag(self, res_list, semkey):
        tok = (semkey, self.cnt[semkey])
        for r in res_list:
            r.w = tok

    def wait_all(self, eng, res_list):
        waits = self._waits(eng, res_list, ())
        self.ops[eng].append((waits, None, None, 0))

    def emit(self):
        nc = self.nc
        sems = self.sems
        ops = self.ops

        def replay(name, e):
            for waits, fn, k, inc in ops[name]:
                for (wk, wv) in waits:
                    e.wait_ge(sems[wk], wv)
                if fn is not None:
                    ins = fn(e)
                    ins.then_inc(sems[k], inc)
        with nc.Block() as block:
            @block.sync
            def _(e):
                replay("sp", e)

            @block.scalar
            def _(e):
                replay("act", e)

            @block.gpsimd
            def _(e):
                replay("pool", e)

            @block.vector
            def _(e):
                replay("dve", e)

            @block.tensor
            def _(e):
                replay("pe", e)


class Pool_:
    def __init__(self, S, name, n, shape, dt):
        self.slots = []
        for i in range(n):
            t = S.es.enter_context(S.nc.sbuf_tensor(f"{name}{i}", shape, dt))
            self.slots.append(Slot(t[:], S.new_sem(f"D_{name}{i}")))
        self.i = 0

    def next(self):
        s = self.slots[self.i % len(self.slots)]
        self.i += 1
        return s


def inherit(dst, srcs):
    for s in srcs:
        toks = list(s.res.r.items())
        if s.res.w is not None:
            toks.append(s.res.w)
        for k, v in toks:
            if dst.res.r.get(k, 0) < v:
                dst.res.r[k] = v


class G:
    pass


def build_program(cfg):
    c = cfg
    D, S_, KC, L, NS = c.D, c.S, c.KC, c.DEPTH, c.NS
    TB1, NSB = c.TB1, c.NSB
    nc = bass.Bass("TRN2", target_bir_lowering=False)
    es = ExitStack()
    S = Sched(nc, es)
    g = G()
    g.c, g.S, g.nc = c, S, nc

    def din(name, shape, dt=F32):
        return nc.dram_tensor(name, shape, dt, kind="ExternalInput").ap()

    def dscr(name, shape, dt=BF16):
        return nc.dram_tensor(name, shape, dt).ap()

    g.xT_in = din("xT", [NS, D, S_])
    g.pos_in = din("pos", [NS, 128, S_], I32)
    NQ = c.NH + c.NH // 2
    wf = dict(win=din("win", [L, c.NCH, 128, KC * 128]), wq=din("wq", [L, NQ, 128, c.QC * 128]),
              wk=din("wk", [L, c.NH, 128, c.KVC * 128]), wv=din("wv", [L, 128, c.KVC * D]),
              wo=din("wo", [L, KC, 128, KC * 128]), wp=din("wp", [L, KC, 128, KC * 128]),
              wout=din("wout", [L, KC, 128, KC * 128]))
    vec_in = din("vec", [128, c.NV])
    g.wdw_in = din("wdw", [L, 128, KC * c.CK])
    ident_in = din("ident", [128, 128], BF16)
    perm_in = din("perm", [128, 128])
    masks_in = din("masks", [128, 4 * 512], BF16)
    onesb_in = din("ones_b", [128, 128], BF16)
    onesf_in = din("ones_f", [128, 128])
    zerosb_in = din("zeros_b", [128, 32], BF16)
    g.out_ap = nc.dram_tensor("out", [NS, D, S_], F32, kind="ExternalOutput").ap()

    shp = dict(win=[c.NCH, 128, KC * 128], wq=[NQ, 128, c.QC * 128], wk=[c.NH, 128, c.KVC * 128],
               wv=[128, c.KVC * D], wo=[KC, 128, KC * 128], wp=[KC, 128, KC * 128], wout=[KC, 128, KC * 128])
    g.wb = {k: [dscr(f"{k}_b{l}", v) for l in range(L)] for k, v in shp.items()}

    g.xs = dscr("xs", [D, S_], F32)
    g.cosT = dscr("cosT", [128, S_], F32)
    g.sinT = dscr("sinT", [128, S_], F32)
    g.cq_s = dscr("cq_s", [c.QL, S_])
    g.ckv_s = dscr("ckv_s", [c.KVL, S_])
    g.krr_s = dscr("krr_s", [128, S_])
    g.gm_s = dscr("gm_s", [D, S_])
    g.u_s = dscr("u_s", [D, 32 + S_])
    g.gc_s = dscr("gc_s", [D, S_])
    g.ma_s = dscr("ma_s", [D, S_])
    g.mc_s = dscr("mc_s", [D, S_])
    g.kn_s = dscr("kn_s", [D, S_])
    g.v_s = dscr("v_s", [S_, D])
    g.qn_s = dscr("qn_s", [D, S_])
    g.qr_s = dscr("qr_s", [D // 2, S_])
    g.ya_s = dscr("ya_s", [D, S_])
    g.cv_s = dscr("cv_s", [D, S_])
    g.yc_s = dscr("yc_s", [D, S_])

    def sb(name, shape, dt):
        return es.enter_context(nc.sbuf_tensor(name, shape, dt))
    ps = es.enter_context(nc.psum_tensor("ps", [128, 4096], F32))
    g.PB = [Res() for _ in range(8)]
    g.bank = lambda b, n=1: ps[:, b * 512:(b + n) * 512]

    S.new_sem("D_cst")
    S.new_sem("D_x")
    S.new_sem("D_uz")
    CR = g.CR = Res()
    g.vec = sb("vec_sb", [128, c.NV], F32)
    g.ident = sb("ident_sb", [128, 128], BF16)
    g.perm = sb("perm_sb", [128, 128], F32)
    g.masks = sb("masks_sb", [128, 4, 512], BF16)
    g.onesb = sb("onesb_sb", [128, 128], BF16)
    g.onesf = sb("onesf_sb", [128, 128], F32)
    g.zerosb = sb("zerosb_sb", [128, 32], BF16)
    S.dmas("sp", [(g.vec[:], vec_in), (g.ident[:], ident_in), (g.perm[:], perm_in), (g.onesb[:], onesb_in),
                  (g.onesf[:], onesf_in), (g.zerosb[:], zerosb_in),
                  (g.masks[:], masks_in.rearrange("p (d f) -> p d f", d=4))], "D_cst", w=[CR])

    g.WR = {}
    g.WSTEP = step = max(1, c.NCH // 8)
    g.p1done = [Res() for _ in range(L)]

    def emit_cast(l):
        prev = []
        k2 = S.new_sem(f"C_ot{l}")
        r2 = g.WR[("oth", l)] = Res()
        ranges = [(j0, min(c.NCH, j0 + step)) for j0 in range(0, c.NCH, step)]
        for pi_, (j0, j1) in enumerate(ranges):
            kk = S.new_sem(f"C_in0_{pi_}")
            rr = g.WR[("win", 0, pi_)] = Res()
            S.dmas("pool", [(g.wb["win"][0][j0:j1], wf["win"][0, j0:j1])], kk, r=prev, w=[rr])
            prev = [rr]
        S.dmas("pool", [(g.wb[k][l], wf[k][l]) for k in ("wk", "wv", "wq", "wo", "wp", "wout")], k2, r=prev, w=[r2])

    def cast_pieces(l):
        kk = S.new_sem(f"C_in{l}")
        k2 = S.new_sem(f"C_ot{l}")
        rr = Res()
        r2 = g.WR[("oth", l)] = Res()
        ranges = [(j0, min(c.NCH, j0 + step)) for j0 in range(0, c.NCH, step)]
        for pi_ in range(len(ranges)):
            g.WR[("win", l, pi_)] = rr
        out = []
        for i, (j0, j1) in enumerate(ranges):
            last = (i == len(ranges) - 1)
            out.append(lambda j0=j0, j1=j1, last=last: S.dmas(
                "pool", [(g.wb["win"][l][j0:j1], wf["win"][l, j0:j1])], kk, w=([rr] if last else [])))
        oth = ("wq", "wk", "wv", "wo", "wp", "wout")
        for i, k in enumerate(oth):
            last = (i == len(oth) - 1)
            out.append(lambda k=k, last=last: S.dmas("pool", [(g.wb[k][l], wf[k][l])], k2,
                                                     w=([r2] if last else [])))
        return out
    g.cast_pieces = cast_pieces
    g.emit_cast = emit_cast
    g.cast_done = set()
    emit_cast(0)
    g.cast_done.add(0)

    BSZ = max(KC * TB1 // 2, 3 * S_, KC * 512, 32 + S_, c.NH * c.KVC * 128, c.KVC * D, max(c.KVC, c.QC) * TB1)
    MSZ = max(2 * S_, 2 * c.CK * 128)
    g.BSZ, g.MSZ = BSZ, MSZ

    def view(ap, name):
        return Slot(ap, S.new_sem("D_" + name))
    B = [sb(f"B{i}", [128, BSZ], BF16) for i in range(3)]
    M0 = sb("M0", [128, MSZ], BF16)
    M1 = sb("M1", [128, S_], BF16)
    g.vB = [view(B[i][:], f"B{i}") for i in range(3)]
    g.vB2lo = view(B[2][:, 0:S_], "B2lo")
    g.vB2hi = view(B[2][:, BSZ // 2:BSZ // 2 + S_], "B2hi") if BSZ // 2 >= S_ else None
    g.vM0 = view(M0[:, 0:S_], "M0")
    g.vM0a = view(M0[:, 0:c.CK * 128], "M0a")
    g.vM0b = view(M0[:, MSZ // 2:MSZ // 2 + c.CK * 128], "M0b")
    g.vM1 = view(M1[:], "M1")
    g.vM0hi = view(M0[:, MSZ // 2:MSZ // 2 + S_], "M0hi")
    g.wsl = Pool_(S, "wsl", 3, [128, KC * 128], BF16)
    g.stg = Pool_(S, "stg", 4, [128, TB1], BF16)
    g.f32a = Pool_(S, "f32a", 3, [128, TB1], F32)
    g.sml = Pool_(S, "sml", 4, [128, 512], F32)
    g.smb = Pool_(S, "smb", 6, [128, 512], BF16)
    g.rsb = view(sb("rsb", [128, TB1], F32)[:], "rsb")
    g.wdb = view(sb("wdb", [128, KC * c.CK], F32)[:], "wdb")
    g.st3 = [view(sb(f"st3_{i}", [128, 512], F32)[:], f"st3_{i}") for i in range(3)]

    def VL(l, off):
        b = c.V_L + l * c.V_PER + off
        return g.vec[:, b:b + 1]
    g.VL = VL

    def store(dst_ap, slot, src_ap, wres, q="act"):
        S.dma(q, dst_ap, src_ap, slot.sem, r=[slot.res], w=wres)

    def load(slot, pairs, rres, split=1):
        out = []
        for d_, s_ in pairs:
            if split == 1:
                out.append((d_, s_))
            else:
                n = d_.shape[1]
                st = n // split
                for i in range(split):
                    out.append((d_[:, i * st:(i + 1) * st], s_[:, i * st:(i + 1) * st]))
        S.dmas("sp", out, slot.sem, r=rres, w=[slot.res])
    g.store, g.load = store, load

    def rstd_from_banks(b0, nb, n_feat, dst):
        d_ = dst.t[:, 0:nb * 512]
        S.op("dve", lambda e: e.tensor_scalar(out=d_, in0=g.bank(b0, nb), scalar1=1.0 / n_feat, scalar2=EPS,
                                              op0=ALU.mult, op1=ALU.add), r=g.PB[b0:b0 + nb], w=[dst.res])
        S.op("act", lambda e: e.activation(out=d_, in_=d_, func=AF.Sqrt), r=[dst.res], w=[dst.res])
        S.op("dve", lambda e: e.reciprocal(out=d_, in_=d_), r=[dst.res], w=[dst.res])
    g.rstd_from_banks = rstd_from_banks
    g.pcount = 0

    def pslot():
        n = 8 // NSB
        b0 = (g.pcount % n) * NSB
        g.pcount += 1
        return b0
    g.pslot = pslot

    for s in range(NS):
        g.X = [[S.R("x", kc, tb) for tb in range(c.NB)] for kc in range(KC)]
        g.xres = lambda kc, t0, t1: [g.X[kc][tb] for tb in range(t0 // 512, t1 // 512)]
        g.xsrc = [g.xT_in[s]] + [g.xs] * (L - 1)
        if s == 0:
            S.dmas("sp", [(g.u_s[kc * 128:(kc + 1) * 128, 0:32], g.zerosb[:]) for kc in range(KC)], "D_uz", r=[CR],
                   w=[S.R("u", kc) for kc in range(KC)])
        for tb in range(c.NTB1):
            t0 = tb * TB1
            pi = g.rsb
            S.dma("sp", pi.t.bitcast(I32), g.pos_in[s, :, t0:t0 + TB1], pi.sem, w=[pi.res])
            A, Bf, Cr = g.f32a.next(), g.f32a.next(), g.f32a.next()
            TWO_PI = 2.0 * math.pi
            C1 = 6.28125
            C2 = TWO_PI - C1

            def dv(fn, r, w):
                S.op("dve", fn, r=r, w=w)
            dv(lambda e: e.tensor_copy(out=A.t, in_=pi.t.bitcast(I32)), [pi.res], [A.res])
            dv(lambda e: e.tensor_scalar(out=A.t, in0=A.t, scalar1=g.vec[:, c.V_INVF:c.V_INVF + 1], scalar2=None,
                                         op0=ALU.mult), [A.res, CR], [A.res])
            dv(lambda e: e.tensor_scalar(out=pi.t.bitcast(I32), in0=A.t, scalar1=1.0 / TWO_PI, scalar2=None,
                                         op0=ALU.mult), [A.res], [pi.res])
            dv(lambda e: e.tensor_copy(out=Bf.t, in_=pi.t.bitcast(I32)), [pi.res], [Bf.res])
            dv(lambda e: e.scalar_tensor_tensor(out=Cr.t, in0=Bf.t, scalar=-C1, in1=A.t, op0=ALU.mult,
                                                op1=ALU.add), [Bf.res, A.res], [Cr.res])
            dv(lambda e: e.scalar_tensor_tensor(out=Cr.t, in0=Bf.t, scalar=-C2, in1=Cr.t, op0=ALU.mult,
                                                op1=ALU.add), [Bf.res, Cr.res], [Cr.res])

            def wrap(X_, M_):
                dv(lambda e: e.tensor_scalar(out=M_.t, in0=X_.t, scalar1=-math.pi, scalar2=None, op0=ALU.is_lt),
                   [X_.res], [M_.res])
                dv(lambda e: e.scalar_tensor_tensor(out=X_.t, in0=M_.t, scalar=TWO_PI, in1=X_.t, op0=ALU.mult,
                                                    op1=ALU.add), [M_.res, X_.res], [X_.res])
                dv(lambda e: e.tensor_scalar(out=M_.t, in0=X_.t, scalar1=math.pi, scalar2=None, op0=ALU.is_gt),
                   [X_.res], [M_.res])
                dv(lambda e: e.scalar_tensor_tensor(out=X_.t, in0=M_.t, scalar=-TWO_PI, in1=X_.t, op0=ALU.mult,
                                                    op1=ALU.add), [M_.res, X_.res], [X_.res])
            wrap(Cr, Bf)
            dv(lambda e: e.tensor_scalar(out=A.t, in0=Cr.t, scalar1=0.5 * math.pi, scalar2=None, op0=ALU.add),
               [Cr.res], [A.res])
            wrap(A, Bf)
            for (dst, ag) in ((g.sinT, Cr), (g.cosT, A)):
                S.op("act", lambda e, ag=ag: e.activation(out=ag.t, in_=ag.t, func=AF.Sin), r=[ag.res], w=[ag.res])
                store(dst[:, t0:t0 + TB1], ag, ag.t, [S.R("rope")])

        for l in range(L):
            layer(g, l)

        for tb in range(c.NTB1):
            t0 = tb * TB1
            rs = norm_stats(g, lambda kc: g.xs[kc * 128:(kc + 1) * 128, t0:t0 + TB1],
                            lambda kc: g.xres(kc, t0, t0 + TB1), KC, D, F32)
            for kc in range(KC):
                xi = g.f32a.next()
                load(xi, [(xi.t, g.xs[kc * 128:(kc + 1) * 128, t0:t0 + TB1])], g.xres(kc, t0, t0 + TB1))
                S.op("dve", lambda e, xi=xi, kc=kc, rs=rs: e.scalar_tensor_tensor(
                    out=xi.t, in0=xi.t, scalar=g.vec[:, c.V_GFIN + kc:c.V_GFIN + kc + 1], in1=rs.t,
                    op0=ALU.mult, op1=ALU.mult), r=[xi.res, rs.res, CR], w=[xi.res])
                store(g.out_ap[s, kc * 128:(kc + 1) * 128, t0:t0 + TB1], xi, xi.t, [S.R("out", kc, tb)])
    S.wait_all("sp", [S.R("out", kc, tb) for kc in range(KC) for tb in range(c.NTB1)])
    S.emit()
    es.close()
    return nc


def norm_stats(g, src_fn, res_fn, nchunk, nfeat, dt):
    c, S = g.c, g.S
    NSB = c.NSB
    for kc in range(nchunk):
        xi = (g.f32a if dt == F32 else g.stg).next()
        g.load(xi, [(xi.t, src_fn(kc))], res_fn(kc))
        sq = g.stg.next()
        S.op("act", lambda e, xi=xi, sq=sq: e.activation(out=sq.t, in_=xi.t, func=AF.Square), r=[xi.res],
             w=[sq.res])

        def f(e, sq=sq, kc=kc):
            for sb_ in range(NSB):
                m = e.matmul(g.bank(sb_), g.onesb[:], sq.t[:, sb_ * 512:(sb_ + 1) * 512], start=(kc == 0),
                             stop=(kc == nchunk - 1))
            return m
        S.op("pe", f, r=[sq.res, g.CR], w=g.PB[0:NSB])
    g.rstd_from_banks(0, NSB, nfeat, g.rsb)
    return g.rsb


def layer(g, l):
    c, S = g.c, g.S
    D, S_, KC, TB1, NSB = c.D, c.S, c.KC, c.TB1, c.NSB
    bank, PB, load, store, CR, VL, WR = g.bank, g.PB, g.load, g.store, g.CR, g.VL, g.WR
    stg, f32a, sml, smb, wsl = g.stg, g.f32a, g.sml, g.smb, g.wsl
    R = S.R
    HALF = KC // 2
    scale = 1.0 / math.sqrt(192.0)
    wb = g.wb
    xsrc = g.xsrc[l]

    def rope(src, t0, dst_ap, dst_res):
        cs = f32a.next()
        load(cs, [(cs.t, g.cosT[:, t0:t0 + TB1])], [R("rope")])
        sn = f32a.next()
        load(sn, [(sn.t, g.sinT[:, t0:t0 + TB1])], [R("rope")])

        def f(e):
            for sb_ in range(NSB):
                m = e.matmul(bank(sb_), g.perm[:], src.t[:, sb_ * 512:(sb_ + 1) * 512], start=True, stop=True)
            return m
        S.op("pe", f, r=[src.res, CR], w=PB[0:NSB])
        S.op("dve", lambda e: e.tensor_tensor(out=sn.t, in0=bank(0, NSB), in1=sn.t, op=ALU.mult),
             r=PB[0:NSB] + [sn.res], w=[sn.res])
        S.op("dve", lambda e: e.tensor_tensor(out=cs.t, in0=src.t, in1=cs.t, op=ALU.mult), r=[src.res, cs.res],
             w=[cs.res])
        o = stg.next()
        S.op("dve", lambda e: e.tensor_tensor(out=o.t, in0=cs.t, in1=sn.t, op=ALU.add), r=[cs.res, sn.res],
             w=[o.res])
        store(dst_ap, o, o.t, dst_res)

    hA, hB = g.vB[0], g.vB[1]

    def hT(kc, a, b):
        sl = hA if kc < HALF else hB
        o = (kc % HALF) * TB1
        return sl.t[:, o + a:o + b]
    for tb in range(c.NTB1):
        t0 = tb * TB1
        rs = norm_stats(g, lambda kc: xsrc[kc * 128:(kc + 1) * 128, t0:t0 + TB1],
                        lambda kc: g.xres(kc, t0, t0 + TB1), KC, D, F32)
        for kc in range(KC):
            xi = f32a.next()
            load(xi, [(xi.t, xsrc[kc * 128:(kc + 1) * 128, t0:t0 + TB1])], g.xres(kc, t0, t0 + TB1))
            S.op("dve", lambda e, xi=xi, kc=kc: e.scalar_tensor_tensor(
                out=hT(kc, 0, TB1), in0=xi.t, scalar=VL(l, c.V_GPRE + kc), in1=rs.t, op0=ALU.mult,
                op1=ALU.mult), r=[xi.res, rs.res, CR], w=[(hA if kc < HALF else hB).res])
        uval = None
        for j, (kind, idx, off) in enumerate(c.chunks):
            ws = wsl.next()
            load(ws, [(ws.t, wb["win"][l][j])], [WR[("win", l, j // g.WSTEP)]])
            b0 = g.pslot()

            def f(e, ws=ws, b0=b0):
                for kc in range(KC):
                    for sb_ in range(NSB):
                        m = e.matmul(bank(b0 + sb_), ws.t[:, kc * 128:(kc + 1) * 128],
                                     hT(kc, sb_ * 512, (sb_ + 1) * 512), start=(kc == 0), stop=(kc == KC - 1))
                return m
            last_ = (tb == c.NTB1 - 1 and j == c.NCH - 1)
            S.op("pe", f, r=[ws.res, hA.res, hB.res], w=PB[b0:b0 + NSB] + ([g.p1done[l]] if last_ else []))
            src = bank(b0, NSB)
            pr = PB[b0:b0 + NSB]
            rows = slice(idx * 128, (idx + 1) * 128)
            if kind == "kr":
                kf = f32a.next()
                S.op("act", lambda e, kf=kf, src=src: e.activation(out=kf.t, in_=src, func=AF.Copy), r=pr,
                     w=[kf.res])
                rope(kf, t0, g.krr_s[:, t0:t0 + TB1], [R("krr")])
                continue
            func = {"cq": AF.Copy, "ckv": AF.Copy, "gm": AF.Silu, "uv": AF.Copy, "ug": AF.Sigmoid, "gc": AF.Silu,
                    "ma": AF.Sigmoid, "mc": AF.Sigmoid}[kind]
            o = stg.next()
            S.op("act", lambda e, o=o, src=src, func=func: e.activation(out=o.t, in_=src, func=func), r=pr,
                 w=[o.res])
            if kind == "uv":
                uval = o
                continue
            if kind == "ug":
                S.op("dve", lambda e, o=o, uv=uval: e.tensor_tensor(out=o.t, in0=uv.t, in1=o.t, op=ALU.mult),
                     r=[uval.res, o.res], w=[o.res])
                store(g.u_s[rows, 32 + t0:32 + t0 + TB1], o, o.t, [R("u", idx)])
                continue
            dst, rk = {"cq": (g.cq_s, ("cq", idx)), "ckv": (g.ckv_s, ("ckv", idx)), "gm": (g.gm_s, ("gm", idx)),
                       "gc": (g.gc_s, ("gc", idx)), "ma": (g.ma_s, ("ma", idx)), "mc": (g.mc_s, ("mc", idx))}[kind]
            store(dst[rows, t0:t0 + TB1], o, o.t, [R(*rk)])

    wkS, wvS, lat = g.vB[0], g.vB[1], g.vB[2]
    load(wkS, [(wkS.t[:, 0:c.NH * c.KVC * 128].rearrange("p (h x) -> p h x", h=c.NH),
                wb["wk"][l].rearrange("h p x -> p h x"))], [WR[("oth", l)]], split=max(1, c.NH // 8))
    load(wvS, [(wvS.t[:, 0:c.KVC * D], wb["wv"][l])], [WR[("oth", l)]])
    VG = min(NSB, D // 512)
    for tb in range(c.NTB1):
        t0 = tb * TB1
        rs = norm_stats(g, lambda kc: g.ckv_s[kc * 128:(kc + 1) * 128, t0:t0 + TB1], lambda kc: [R("ckv", kc)], c.KVC,
                        c.KVL, BF16)
        for kc in range(c.KVC):
            xi = stg.next()
            load(xi, [(xi.t, g.ckv_s[kc * 128:(kc + 1) * 128, t0:t0 + TB1])], [R("ckv", kc)])
            S.op("dve", lambda e, xi=xi, kc=kc: e.scalar_tensor_tensor(
                out=lat.t[:, kc * TB1:(kc + 1) * TB1], in0=xi.t, scalar=VL(l, c.V_GKV + kc), in1=rs.t,
                op0=ALU.mult, op1=ALU.mult), r=[xi.res, rs.res, CR], w=[lat.res])
        for h in range(c.NH):
            b0 = g.pslot()

            def f(e, h=h, b0=b0):
                for kc in range(c.KVC):
                    for sb_ in range(NSB):
                        o_ = (h * c.KVC + kc) * 128
                        m = e.matmul(bank(b0 + sb_), wkS.t[:, o_:o_ + 128],
                                     lat.t[:, kc * TB1 + sb_ * 512:kc * TB1 + (sb_ + 1) * 512], start=(kc == 0),
                                     stop=(kc == c.KVC - 1))
                return m
            S.op("pe", f, r=[wkS.res, lat.res], w=PB[b0:b0 + NSB])
            o = stg.next()
            S.op("act", lambda e, o=o, b0=b0: e.activation(out=o.t, in_=bank(b0, NSB), func=AF.Copy),
                 r=PB[b0:b0 + NSB], w=[o.res])
            store(g.kn_s[h * 128:(h + 1) * 128, t0:t0 + TB1], o, o.t, [R("kn", h)])
        for tt in range(TB1 // 128):
            for hg in range(D // (512 * VG)):
                b0 = g.pslot()

                def f(e, tt=tt, hg=hg, b0=b0):
                    for q4 in range(VG):
                        for kc in range(c.KVC):
                            o_ = kc * D + (hg * VG + q4) * 512
                            m = e.matmul(bank(b0 + q4), lat.t[:, kc * TB1 + tt * 128:kc * TB1 + (tt + 1) * 128],
                                         wvS.t[:, o_:o_ + 512], start=(kc == 0), stop=(kc == c.KVC - 1))
                    return m
                S.op("pe", f, r=[wvS.res, lat.res], w=PB[b0:b0 + VG])
                o = stg.next()
                S.op("act", lambda e, o=o, b0=b0: e.activation(out=o.t[:, 0:VG * 512], in_=bank(b0, VG),
                                                               func=AF.Copy), r=PB[b0:b0 + VG], w=[o.res])
                store(g.v_s[t0 + tt * 128:t0 + (tt + 1) * 128, hg * VG * 512:(hg + 1) * VG * 512], o,
                      o.t[:, 0:VG * 512], [R("v", t0 // 128 + tt, hg)])
        rs = norm_stats(g, lambda kc: g.cq_s[kc * 128:(kc + 1) * 128, t0:t0 + TB1], lambda kc: [R("cq", kc)], c.QC,
                        c.QL, BF16)
        for kc in range(c.QC):
            xi = stg.next()
            load(xi, [(xi.t, g.cq_s[kc * 128:(kc + 1) * 128, t0:t0 + TB1])], [R("cq", kc)])
            S.op("dve", lambda e, xi=xi, kc=kc: e.scalar_tensor_tensor(
                out=lat.t[:, kc * TB1:(kc + 1) * TB1], in0=xi.t, scalar=VL(l, c.V_GQ + kc), in1=rs.t,
                op0=ALU.mult, op1=ALU.mult), r=[xi.res, rs.res, CR], w=[lat.res])
        for j in range(c.NH + c.NH // 2):
            ws = wsl.next()
            load(ws, [(ws.t[:, 0:c.QC * 128], wb["wq"][l][j])], [WR[("oth", l)]])
            b0 = g.pslot()

            def f(e, ws=ws, b0=b0):
                for kc in range(c.QC):
                    for sb_ in range(NSB):
                        m = e.matmul(bank(b0 + sb_), ws.t[:, kc * 128:(kc + 1) * 128],
                                     lat.t[:, kc * TB1 + sb_ * 512:kc * TB1 + (sb_ + 1) * 512], start=(kc == 0),
                                     stop=(kc == c.QC - 1))
                return m
            S.op("pe", f, r=[ws.res, lat.res], w=PB[b0:b0 + NSB])
            if j < c.NH:
                o = stg.next()
                S.op("act", lambda e, o=o, b0=b0: e.activation(out=o.t, in_=bank(b0, NSB), func=AF.Copy),
                     r=PB[b0:b0 + NSB], w=[o.res])
                store(g.qn_s[j * 128:(j + 1) * 128, t0:t0 + TB1], o, o.t, [R("qn", j)])
            else:
                qf = f32a.next()
                S.op("act", lambda e, qf=qf, b0=b0: e.activation(out=qf.t, in_=bank(b0, NSB), func=AF.Copy),
                     r=PB[b0:b0 + NSB], w=[qf.res])
                jj = j - c.NH
                rope(qf, t0, g.qr_s[jj * 128:(jj + 1) * 128, t0:t0 + TB1], [R("qr", jj)])

    NKT = S_ // 128
    krS = g.vM0
    inherit(krS, [g.vM0a, g.vM0b])
    inherit(g.vM0hi, [g.vM0a, g.vM0b])
    knv = [g.vB2lo] + ([g.vB2hi] if g.vB2hi is not None else [])
    for v_ in knv:
        inherit(v_, [g.vB[2]])
    qrv = [g.vM1, g.vM0hi]
    load(krS, [(krS.t, g.krr_s[:, :])], [R("krr")])

    def p3_loads(h):
        knS = knv[h % len(knv)]
        load(knS, [(knS.t, g.kn_s[h * 128:(h + 1) * 128, :])], [R("kn", h)])
        hb = g.vB[h % 2]
        vsp = max(1, NKT // 8)
        vst = NKT // vsp
        vd = hb.t[:, S_:2 * S_].rearrange("p (k d) -> p k d", d=128)
        vs = g.v_s[:, h * 128:(h + 1) * 128].rearrange("(k p) d -> p k d", p=128)
        load(hb, [(hb.t[:, 0:S_], g.qn_s[h * 128:(h + 1) * 128, :]),
                  (hb.t[:, 2 * S_:3 * S_], g.gm_s[h * 128:(h + 1) * 128, :])] +
             [(vd[:, i * vst:(i + 1) * vst], vs[:, i * vst:(i + 1) * vst]) for i in range(vsp)],
             [R("qn", h), R("gm", h)] + [R("v", i, (h * 128) // (VG * 512)) for i in range(NKT)])
        if h % 2 == 0:
            qr_ = qrv[(h // 2) % 2]
            load(qr_, [(qr_.t, g.qr_s[(h // 2) * 128:(h // 2 + 1) * 128, :])], [R("qr", h // 2)])
    p3_loads(0)
    pieces = []
    if l + 1 < c.DEPTH and (l + 1) not in g.cast_done:
        pieces = g.cast_pieces(l + 1)
        g.cast_done.add(l + 1)
    npts = min(c.NH, 8)
    per_pt = -(-len(pieces) // npts) if pieces else 0
    for h in range(c.NH):
        if pieces and h % (c.NH // npts) == 0:
            for _ in range(per_pt):
                if pieces:
                    pieces.pop(0)()
        if h + 1 < c.NH:
            p3_loads(h + 1)
        knS = knv[h % len(knv)]
        hb = g.vB[h % 2]
        qrS = qrv[(h // 2) % 2]
        pb = (h % 2) * 64
        for qb in range(c.NB):
            nkt = 4 * qb + 4
            OTb = 4 + (qb % 2)
            SUMb = 6 + (qb % 2)
            q0 = qb * 512
            Pt = [None] * nkt

            def st_group(kt, qb=qb, q0=q0, Pt=Pt, knS=knS, hb=hb, pb=pb, qrS=qrS):
                bk = kt % 4

                def f(e):
                    e.matmul(bank(bk), knS.t[:, kt * 128:(kt + 1) * 128], hb.t[:, q0:q0 + 512], start=True,
                             stop=False)
                    return e.matmul(bank(bk), krS.t[pb:pb + 64, kt * 128:(kt + 1) * 128],
                                    qrS.t[pb:pb + 64, q0:q0 + 512], start=False, stop=True)
                S.op("pe", f, r=[knS.res, hb.res, krS.res, qrS.res], w=[PB[bk]])
                p = smb.next()
                Pt[kt] = p
                S.op("act", lambda e: e.activation(out=p.t, in_=bank(bk), func=AF.Exp, scale=scale), r=[PB[bk]],
                     w=[p.res])
                d = kt - 4 * qb
                if d >= 0:
                    S.op("pool", lambda e: e.tensor_tensor(out=p.t, in0=p.t, in1=g.masks[:, d, :], op=ALU.mult),
                         r=[p.res, CR], w=[p.res])

            def pv_group(kt, nkt=nkt, OTb=OTb, SUMb=SUMb, Pt=Pt, hb=hb):
                p = Pt[kt]

                def f(e):
                    e.matmul(bank(OTb), hb.t[:, S_ + kt * 128:S_ + (kt + 1) * 128], p.t, start=(kt == 0),
                             stop=(kt == nkt - 1))
                    return e.matmul(bank(SUMb), g.onesb[:], p.t, start=(kt == 0), stop=(kt == nkt - 1))
                S.op("pe", f, r=[hb.res, p.res, CR], w=[PB[OTb], PB[SUMb]])
            SK = 3
            for i in range(nkt + SK):
                if i < nkt:
                    st_group(i)
                if i >= SK:
                    pv_group(i - SK)
            ri = sml.next()
            S.op("dve", lambda e, ri=ri, SUMb=SUMb: e.reciprocal(out=ri.t, in_=bank(SUMb)), r=[PB[SUMb]],
                 w=[ri.res])
            S.op("dve", lambda e, ri=ri, OTb=OTb: e.tensor_tensor(out=ri.t, in0=bank(OTb), in1=ri.t, op=ALU.mult),
                 r=[PB[OTb], ri.res], w=[ri.res])
            yo = smb.next()
            S.op("dve", lambda e, ri=ri, yo=yo, hb=hb, q0=q0: e.tensor_tensor(
                out=yo.t, in0=ri.t, in1=hb.t[:, 2 * S_ + q0:2 * S_ + q0 + 512], op=ALU.mult), r=[ri.res, hb.res],
                w=[yo.res])
            store(g.ya_s[h * 128:(h + 1) * 128, q0:q0 + 512], yo, yo.t, [R("ya", qb)], q="sp")

    while pieces:
        pieces.pop(0)()

    wd = g.wdb
    load(wd, [(wd.t, g.wdw_in[l])], [])
    inherit(g.vM0a, [g.vM0])
    inherit(g.vM0b, [g.vM0, g.vM0hi])
    CV2 = min(2, c.NB)
    for cc in range(KC):
        dg = g.vM0a if cc % 2 == 0 else g.vM0b

        def fd(e, cc=cc, dg=dg):
            for k in range(c.CK):
                m = e.tensor_scalar(out=dg.t[:, k * 128:(k + 1) * 128], in0=g.ident[:],
                                    scalar1=wd.t[:, cc * c.CK + k:cc * c.CK + k + 1], scalar2=None, op0=ALU.mult)
            return m
        S.op("dve", fd, r=[wd.res, CR], w=[dg.res])
        uu = g.vB[cc % 2]
        load(uu, [(uu.t[:, 0:32 + S_], g.u_s[cc * 128:(cc + 1) * 128, :])], [R("u", cc)])
        for t4 in range(0, c.NB, CV2):
            b0 = g.pslot()

            def f(e, t4=t4, b0=b0, dg=dg, uu=uu):
                for q in range(CV2):
                    ts = (t4 + q) * 512
                    for k in range(c.CK):
                        a = 32 + ts - (c.CK - 1) + k
                        m = e.matmul(bank(b0 + q), dg.t[:, k * 128:(k + 1) * 128], uu.t[:, a:a + 512],
                                     start=(k == 0), stop=(k == c.CK - 1))
                return m
            S.op("pe", f, r=[dg.res, uu.res], w=PB[b0:b0 + CV2])
            o = stg.next()
            S.op("act", lambda e, o=o, b0=b0, cc=cc: e.activation(
                out=o.t[:, 0:CV2 * 512], in_=bank(b0, CV2), func=AF.Identity, bias=VL(l, c.V_BDW + cc)),
                r=PB[b0:b0 + CV2] + [CR], w=[o.res])
            store(g.cv_s[cc * 128:(cc + 1) * 128, t4 * 512:(t4 + CV2) * 512], o, o.t[:, 0:CV2 * 512], [R("cv", cc, t4)])

    inherit(g.vB[2], knv)
    mean, rstd, nmr = g.st3
    for tbk in range(c.NB):
        q0 = tbk * 512
        cvS = g.vB[0] if tbk % 2 == 0 else g.vB[2]
        load(cvS, [(cvS.t[:, 0:KC * 512].rearrange("p (c t) -> p c t", t=512),
                    g.cv_s[:, q0:q0 + 512].rearrange("(c p) t -> p c t", p=128))],
             [R("cv", i, (tbk // CV2) * CV2) for i in range(KC)], split=max(1, KC // 8))
        gcS = g.vB[1]
        load(gcS, [(gcS.t[:, 0:KC * 512].rearrange("p (c t) -> p c t", t=512),
                    g.gc_s[:, q0:q0 + 512].rearrange("(c p) t -> p c t", p=128))], [R("gc", i) for i in range(KC)],
             split=max(1, KC // 8))
        for cc in range(KC):
            sq = smb.next()
            S.op("act", lambda e, cc=cc, sq=sq, cvS=cvS: e.activation(
                out=sq.t, in_=cvS.t[:, cc * 512:(cc + 1) * 512], func=AF.Square), r=[cvS.res], w=[sq.res])

            def f(e, cc=cc, sq=sq, cvS=cvS):
                e.matmul(bank(0), g.onesb[:], cvS.t[:, cc * 512:(cc + 1) * 512], start=(cc == 0),
                         stop=(cc == KC - 1))
                return e.matmul(bank(1), g.onesb[:], sq.t, start=(cc == 0), stop=(cc == KC - 1))
            S.op("pe", f, r=[cvS.res, sq.res, CR], w=PB[0:2])

        S.op("dve", lambda e: e.tensor_scalar(out=mean.t, in0=bank(0), scalar1=1.0 / D, scalar2=None, op0=ALU.mult),
             r=[PB[0]], w=[mean.res])
        S.op("dve", lambda e: e.tensor_tensor(out=nmr.t, in0=mean.t, in1=mean.t, op=ALU.mult), r=[mean.res],
             w=[nmr.res])
        S.op("dve", lambda e: e.scalar_tensor_tensor(out=rstd.t, in0=bank(1), scalar=1.0 / D, in1=nmr.t,
                                                     op0=ALU.mult, op1=ALU.subtract), r=[PB[1], nmr.res],
             w=[rstd.res])
        S.op("dve", lambda e: e.tensor_scalar(out=rstd.t, in0=rstd.t, scalar1=EPS, scalar2=None, op0=ALU.add),
             r=[rstd.res], w=[rstd.res])
        S.op("act", lambda e: e.activation(out=rstd.t, in_=rstd.t, func=AF.Sqrt), r=[rstd.res], w=[rstd.res])
        S.op("dve", lambda e: e.reciprocal(out=rstd.t, in_=rstd.t), r=[rstd.res], w=[rstd.res])
        S.op("dve", lambda e: e.scalar_tensor_tensor(out=nmr.t, in0=mean.t, scalar=-1.0, in1=rstd.t, op0=ALU.mult,
                                                     op1=ALU.mult), r=[mean.res, rstd.res], w=[nmr.res])
        for cc in range(KC):
            t1 = sml.next()

            S.op("dve", lambda e, cc=cc, t1=t1, cvS=cvS: e.tensor_tensor(
                out=t1.t, in0=cvS.t[:, cc * 512:(cc + 1) * 512], in1=rstd.t, op=ALU.mult), r=[cvS.res, rstd.res],
                w=[t1.res])
            S.op("dve", lambda e, t1=t1: e.tensor_tensor(out=t1.t, in0=t1.t, in1=nmr.t, op=ALU.add),
                 r=[t1.res, nmr.res], w=[t1.res])
            S.op("act", lambda e, cc=cc, t1=t1: e.activation(out=t1.t, in_=t1.t, func=AF.Silu,
                                                             scale=VL(l, c.V_GCN + cc), bias=VL(l, c.V_BCN + cc)),
                 r=[t1.res, CR], w=[t1.res])
            yo = smb.next()
            S.op("pool", lambda e, cc=cc, t1=t1, yo=yo: e.tensor_tensor(
                out=yo.t, in0=t1.t, in1=gcS.t[:, cc * 512:(cc + 1) * 512], op=ALU.mult), r=[t1.res, gcS.res],
                w=[yo.res])
            store(g.yc_s[cc * 128:(cc + 1) * 128, q0:q0 + 512], yo, yo.t, [R("yc", tbk)], q="pool")

    yaS, ycS, yS = g.vB[0], g.vB[1], g.vB[2]
    for tbk in range(c.NB):
        q0 = tbk * 512
        load(yaS, [(yaS.t[:, 0:KC * 512].rearrange("p (c t) -> p c t", t=512),
                    g.ya_s[:, q0:q0 + 512].rearrange("(c p) t -> p c t", p=128))], [R("ya", tbk)],
             split=max(1, KC // 8))
        load(ycS, [(ycS.t[:, 0:KC * 512].rearrange("p (c t) -> p c t", t=512),
                    g.yc_s[:, q0:q0 + 512].rearrange("(c p) t -> p c t", p=128))], [R("yc", tbk)],
             split=max(1, KC // 8))
        for m_ in range(KC):
            for src_, wkey, bk in ((yaS, "wo", 0), (ycS, "wp", 1)):
                ws = wsl.next()
                load(ws, [(ws.t, wb[wkey][l][m_])], [WR[("oth", l)]])

                def f(e, ws=ws, src_=src_, bk=bk):
                    for kc in range(KC):
                        m = e.matmul(bank(bk), ws.t[:, kc * 128:(kc + 1) * 128], src_.t[:, kc * 512:(kc + 1) * 512],
                                     start=(kc == 0), stop=(kc == KC - 1))
                    return m
                S.op("pe", f, r=[ws.res, src_.res], w=[PB[bk]])
            gA, gC = smb.next(), smb.next()
            load(gA, [(gA.t, g.ma_s[m_ * 128:(m_ + 1) * 128, q0:q0 + 512])], [R("ma", m_)])
            load(gC, [(gC.t, g.mc_s[m_ * 128:(m_ + 1) * 128, q0:q0 + 512])], [R("mc", m_)])
            t1, t2 = sml.next(), sml.next()
            S.op("dve", lambda e, t1=t1, gA=gA: e.tensor_tensor(out=t1.t, in0=bank(0), in1=gA.t, op=ALU.mult),
                 r=[PB[0], gA.res], w=[t1.res])
            S.op("dve", lambda e, t2=t2, gC=gC: e.tensor_tensor(out=t2.t, in0=bank(1), in1=gC.t, op=ALU.mult),
                 r=[PB[1], gC.res], w=[t2.res])
            S.op("dve", lambda e, t1=t1, t2=t2, m_=m_: e.tensor_tensor(out=yS.t[:, m_ * 512:(m_ + 1) * 512],
                                                                       in0=t1.t, in1=t2.t, op=ALU.add),
                 r=[t1.res, t2.res], w=[yS.res])
        for m_ in range(KC):
            ws = wsl.next()
            load(ws, [(ws.t, wb["wout"][l][m_])], [WR[("oth", l)]])
            bk = 2 + (m_ % 2)

            def f(e, ws=ws, bk=bk):
                for kc in range(KC):
                    m = e.matmul(bank(bk), ws.t[:, kc * 128:(kc + 1) * 128], yS.t[:, kc * 512:(kc + 1) * 512],
                                 start=(kc == 0), stop=(kc == KC - 1))
                return m
            S.op("pe", f, r=[ws.res, yS.res], w=[PB[bk]])
            xi = sml.next()
            load(xi, [(xi.t, xsrc[m_ * 128:(m_ + 1) * 128, q0:q0 + 512])], [g.X[m_][tbk]])
            S.op("dve", lambda e, xi=xi, bk=bk: e.tensor_tensor(out=xi.t, in0=bank(bk), in1=xi.t, op=ALU.add),
                 r=[PB[bk], xi.res], w=[xi.res])
            store(g.xs[m_ * 128:(m_ + 1) * 128, q0:q0 + 512], xi, xi.t, [g.X[m_][tbk]])


def run(cfg, inputs, trace=False):
    per = host_prep(cfg, inputs)
    nc = build_program(cfg)
    res = run_bass_kernel_spmd(nc, per, core_ids=list(range(cfg.ncores)), trace=trace)
    outs = []
    for core in range(cfg.ncores):
        o = res.results[core]["out"]
        for i in range(cfg.NS):
            outs.append(np.ascontiguousarray(o[i].T))
    out = np.stack(outs).astype(np.float32)
    return (out, res) if trace else out


def kernel(**inputs):
    return run(Cfg(), inputs)
```
